# Optimizing a Trainium2 kernel written in Bass

```python
import math
import jax, jax.numpy as jnp
from jax import lax
import numpy as np

D_MODEL = 1024
BATCH = 16
SEQ = 4096
DEPTH = 4

CHUNK = 64
SSM_WIDTH = D_MODEL // 2
SSM_GROUP = 16
SSM_GROUPS = SSM_WIDTH // SSM_GROUP
SSM_STATE = 64
DT_MIN = 1e-3
DT_MAX = 1e-1
ATT_WIDTH = D_MODEL // 2
ATT_HEAD_DIM = 64
ATT_HEADS = ATT_WIDTH // ATT_HEAD_DIM
LEFT_CHUNKS = 8
BAND = (LEFT_CHUNKS + 1) * CHUNK
MAX_REL = 128
N_REL = 2 * MAX_REL + 1
D_FF = ((8 * D_MODEL // 3 + 127) // 128) * 128
IN_WIDTH = SSM_WIDTH + 3 * ATT_WIDTH + 2 * D_MODEL
RMS_EPS = 1e-6
MASK_VALUE = -1e30

kernel_name = "hybrid_s5_chunkattn_macaron_sandwich"


def rmsnorm(x, g):
    xf = x.astype(jnp.float32)
    y = xf * lax.rsqrt(jnp.mean(xf * xf, axis=-1, keepdims=True) + RMS_EPS)
    return (y * g.astype(jnp.float32)).astype(x.dtype)


def swiglu(h, w_gate, w_up, w_down):
    return (jax.nn.silu(h @ w_gate) * (h @ w_up)) @ w_down


def s5_scan(u, lam_re, lam_im, log_dt, b_re, b_im, c_re, c_im, d_skip):
    dtype = u.dtype
    bsz, seq, _ = u.shape
    f32 = jnp.float32
    uf = u.astype(f32).reshape(bsz, seq, SSM_GROUPS, SSM_GROUP)
    lr, li = lam_re.astype(f32), lam_im.astype(f32)
    dt = jnp.exp(log_dt.astype(f32))[:, None]
    mag = jnp.exp(lr * dt)
    ang = li * dt
    ab_re, ab_im = mag * jnp.cos(ang), mag * jnp.sin(ang)
    nr, ni = ab_re - 1.0, ab_im
    den = lr * lr + li * li
    f_re = (nr * lr + ni * li) / den
    f_im = (ni * lr - nr * li) / den
    br, bi = b_re.astype(f32), b_im.astype(f32)
    bb_re = f_re[..., None] * br - f_im[..., None] * bi
    bb_im = f_re[..., None] * bi + f_im[..., None] * br
    bu_re = jnp.einsum('bsgh,gph->bsgp', uf, bb_re)
    bu_im = jnp.einsum('bsgh,gph->bsgp', uf, bb_im)
    a_re = jnp.broadcast_to(ab_re, (1, seq, SSM_GROUPS, SSM_STATE))
    a_im = jnp.broadcast_to(ab_im, (1, seq, SSM_GROUPS, SSM_STATE))

    def combine(left, right):
        la_re, la_im, lb_re, lb_im = left
        ra_re, ra_im, rb_re, rb_im = right
        return (ra_re * la_re - ra_im * la_im,
                ra_re * la_im + ra_im * la_re,
                ra_re * lb_re - ra_im * lb_im + rb_re,
                ra_re * lb_im + ra_im * lb_re + rb_im)

    _, _, s_re, s_im = lax.associative_scan(combine, (a_re, a_im, bu_re, bu_im), axis=1)
    y = (jnp.einsum('bsgp,ghp->bsgh', s_re, c_re.astype(f32))
         - jnp.einsum('bsgp,ghp->bsgh', s_im, c_im.astype(f32))
         + d_skip.astype(f32) * uf)
    return y.reshape(bsz, seq, SSM_WIDTH).astype(dtype)


def chunk_attention(q, k, v, rel_bias):
    bsz, seq = q.shape[0], q.shape[1]
    n_chunks = seq // CHUNK
    pad = LEFT_CHUNKS * CHUNK
    kp = jnp.pad(k, ((0, 0), (pad, 0), (0, 0), (0, 0)))
    vp = jnp.pad(v, ((0, 0), (pad, 0), (0, 0), (0, 0)))
    qc = q.reshape(bsz, n_chunks, CHUNK, ATT_HEADS, ATT_HEAD_DIM).transpose(1, 0, 2, 3, 4)
    rel = (jnp.arange(CHUNK)[:, None] + pad) - jnp.arange(BAND)[None, :]
    rel_idx = jnp.clip(rel, -MAX_REL, MAX_REL) + MAX_REL
    bias = rel_bias.astype(jnp.float32)[:, rel_idx]
    scale = ATT_HEAD_DIM ** -0.5

    def one_chunk(args):
        c, q_blk = args
        start = c * CHUNK
        k_band = lax.dynamic_slice_in_dim(kp, start, BAND, axis=1)
        v_band = lax.dynamic_slice_in_dim(vp, start, BAND, axis=1)
        s = jnp.einsum('bqhd,bkhd->bhqk', q_blk, k_band).astype(jnp.float32) * scale + bias
        valid = (start - pad + jnp.arange(BAND)) >= 0
        s = jnp.where(valid, s, MASK_VALUE)
        p = jax.nn.softmax(s, axis=-1).astype(v.dtype)
        return jnp.einsum('bhqk,bkhd->bqhd', p, v_band)

    out = lax.map(one_chunk, (jnp.arange(n_chunks), qc))
    return out.transpose(1, 0, 2, 3, 4).reshape(bsz, seq, ATT_WIDTH)


def hybrid_mixer(h, w_in, lam_re, lam_im, log_dt, b_re, b_im, c_re, c_im, d_skip,
                 w_glu_val, w_glu_gate, w_out_ssm, rel_bias, w_out_att, w_o):
    bsz, seq, _ = h.shape
    proj = h @ w_in
    o1 = SSM_WIDTH
    o2 = o1 + ATT_WIDTH
    o3 = o2 + ATT_WIDTH
    o4 = o3 + ATT_WIDTH
    o5 = o4 + D_MODEL
    u, q, k, v, g_a, g_b = jnp.split(proj, [o1, o2, o3, o4, o5], axis=-1)
    y_a = jax.nn.gelu(s5_scan(u, lam_re, lam_im, log_dt, b_re, b_im, c_re, c_im, d_skip))
    y_a = ((y_a @ w_glu_val) * jax.nn.sigmoid(y_a @ w_glu_gate)) @ w_out_ssm
    hs = (bsz, seq, ATT_HEADS, ATT_HEAD_DIM)
    y_b = chunk_attention(q.reshape(hs), k.reshape(hs), v.reshape(hs), rel_bias) @ w_out_att
    merged = jax.nn.sigmoid(g_a) * y_a + jax.nn.sigmoid(g_b) * y_b
    return merged @ w_o


def setup_inputs(seed: int = 0) -> dict:
    key = jax.random.key(seed)
    ks = jax.random.split(key, 20)
    f32 = jnp.float32
    L, G, P, H = DEPTH, SSM_GROUPS, SSM_STATE, SSM_GROUP

    def nrm(k, shape, scale):
        return jax.random.normal(k, shape, f32) * scale

    x = jax.random.normal(ks[0], (BATCH, SEQ, D_MODEL), f32)
    norm_gains = 1.0 + nrm(ks[1], (L, 6, D_MODEL), 0.05)
    ffn_w_gate = nrm(ks[2], (L, 2, D_MODEL, D_FF), D_MODEL ** -0.5)
    ffn_w_up = nrm(ks[3], (L, 2, D_MODEL, D_FF), D_MODEL ** -0.5)
    ffn_w_down = nrm(ks[4], (L, 2, D_FF, D_MODEL), D_FF ** -0.5)
    w_in = nrm(ks[5], (L, D_MODEL, IN_WIDTH), D_MODEL ** -0.5)
    lam_re = -0.5 + nrm(ks[6], (L, G, P), 0.01)
    lam_im = jnp.pi * jnp.arange(P, dtype=f32) + nrm(ks[7], (L, G, P), 0.01)
    log_dt = math.log(DT_MIN) + jax.random.uniform(ks[8], (L, G), f32) * (math.log(DT_MAX) - math.log(DT_MIN))
    b_re = nrm(ks[9], (L, G, P, H), (2 * H) ** -0.5)
    b_im = nrm(ks[10], (L, G, P, H), (2 * H) ** -0.5)
    c_re = nrm(ks[11], (L, G, H, P), P ** -0.5)
    c_im = nrm(ks[12], (L, G, H, P), P ** -0.5)
    d_skip = nrm(ks[13], (L, G, H), 1.0)
    w_glu_val = nrm(ks[14], (L, SSM_WIDTH, SSM_WIDTH), SSM_WIDTH ** -0.5)
    w_glu_gate = nrm(ks[15], (L, SSM_WIDTH, SSM_WIDTH), SSM_WIDTH ** -0.5)
    w_out_ssm = nrm(ks[16], (L, SSM_WIDTH, D_MODEL), SSM_WIDTH ** -0.5)
    rel_bias = nrm(ks[17], (L, ATT_HEADS, N_REL), 0.1)
    w_out_att = nrm(ks[18], (L, ATT_WIDTH, D_MODEL), ATT_WIDTH ** -0.5)
    w_o = nrm(ks[19], (L, D_MODEL, D_MODEL), D_MODEL ** -0.5)
    return {"x": x, "norm_gains": norm_gains, "ffn_w_gate": ffn_w_gate,
            "ffn_w_up": ffn_w_up, "ffn_w_down": ffn_w_down, "w_in": w_in,
            "lam_re": lam_re, "lam_im": lam_im, "log_dt": log_dt,
            "b_re": b_re, "b_im": b_im, "c_re": c_re, "c_im": c_im,
            "d_skip": d_skip, "w_glu_val": w_glu_val, "w_glu_gate": w_glu_gate,
            "w_out_ssm": w_out_ssm, "rel_bias": rel_bias, "w_out_att": w_out_att,
            "w_o": w_o}


def reference(x, norm_gains, ffn_w_gate, ffn_w_up, ffn_w_down, w_in,
              lam_re, lam_im, log_dt, b_re, b_im, c_re, c_im, d_skip,
              w_glu_val, w_glu_gate, w_out_ssm, rel_bias, w_out_att, w_o):
    for l in range(DEPTH):
        g = norm_gains[l]
        f1 = swiglu(rmsnorm(x, g[0]), ffn_w_gate[l, 0], ffn_w_up[l, 0], ffn_w_down[l, 0])
        x = x + 0.5 * rmsnorm(f1, g[1])
        m = hybrid_mixer(rmsnorm(x, g[2]), w_in[l], lam_re[l], lam_im[l], log_dt[l],
                         b_re[l], b_im[l], c_re[l], c_im[l], d_skip[l],
                         w_glu_val[l], w_glu_gate[l], w_out_ssm[l],
                         rel_bias[l], w_out_att[l], w_o[l])
        x = x + rmsnorm(m, g[3])
        f2 = swiglu(rmsnorm(x, g[4]), ffn_w_gate[l, 1], ffn_w_up[l, 1], ffn_w_down[l, 1])
        x = x + 0.5 * rmsnorm(f2, g[5])
    return x
```

```python
from contextlib import ExitStack
import math
import os
import numpy as np
import concourse.bass as bass
import concourse.mybir as mybir
from concourse.bass_utils import run_bass_kernel_spmd

F32 = mybir.dt.float32
BF16 = mybir.dt.bfloat16
AF = mybir.ActivationFunctionType
ALU = mybir.AluOpType
AX = mybir.AxisListType

D = 1024
DFF = 2816
NF = DFF // 128
KT = D // 128
T = 512
NB = T // 128
EPS = 1e-6
N_CORES = 8


class Buf:
    __slots__ = ("name", "last_w", "readers", "dsem", "dcnt")

    def __init__(self, name):
        self.name = name
        self.last_w = {}
        self.readers = {}
        self.dsem = None
        self.dcnt = 0


class Eng:
    def __init__(self, fw, eng, name, own_wait):
        self.fw = fw
        self.eng = eng
        self.name = name
        self.sem = fw.new_sem("e_" + name)
        self.cnt = 0
        self.seen = {}
        self.own_wait = own_wait
        self.pend_r = []
        self.pend_w = []

    def wait(self, key, sem, val):
        if sem is self.sem and not self.own_wait:
            return
        if self.seen.get(key, 0) < val:
            self.eng.wait_ge(sem, val)
            self.seen[key] = val
            self.fw.nwaits += 1


class FW:
    def __init__(self, nc, es):
        self.nc = nc
        self.es = es
        self.nsem = 0
        self.nwaits = 0
        self.ninst = 0
        self.pe = Eng(self, nc.tensor, "pe", False)
        self.act = Eng(self, nc.scalar, "act", True)
        self.dve = Eng(self, nc.vector, "dve", True)
        self.pool = Eng(self, nc.gpsimd, "pool", True)
        self.sp = Eng(self, nc.sync, "sp", False)
        self.dma_bufs = []

    def new_sem(self, name):
        self.nsem += 1
        return self.es.enter_context(self.nc.semaphore(f"{name}_{self.nsem}"))

    def _deps(self, E, reads, writes):
        for b in reads:
            for k, (s, v) in b.last_w.items():
                E.wait(k, s, v)
        for b in writes:
            for k, (s, v) in b.last_w.items():
                E.wait(k, s, v)
            for k, (s, v) in b.readers.items():
                E.wait(k, s, v)

    def op(self, E, fn, reads=(), writes=(), flag=True):
        self._deps(E, reads, writes)
        ins = fn()
        self.ninst += 1
        if flag:
            E.cnt += 1
            ins.then_inc(E.sem, 1)
            key = id(E.sem)
            ev = (E.sem, E.cnt)
            for b in E.pend_r:
                b.readers[key] = ev
            for b in reads:
                b.readers[key] = ev
            for b in E.pend_w:
                b.last_w = {key: ev}
                b.readers = {}
            for b in writes:
                b.last_w = {key: ev}
                b.readers = {}
            E.pend_r = []
            E.pend_w = []
        else:
            E.pend_r.extend(reads)
            E.pend_w.extend(writes)
        return ins

    def dma(self, Q, out, in_, sb, reads=(), writes=()):
        self._deps(Q, reads, writes)
        if sb.dsem is None:
            sb.dsem = self.new_sem("d_" + sb.name)
            self.dma_bufs.append(sb)
        ins = Q.eng.dma_start(out=out, in_=in_)
        sb.dcnt += 16
        ins.then_inc(sb.dsem, 16)
        self.ninst += 1
        key = id(sb.dsem)
        ev = (sb.dsem, sb.dcnt)
        for b in reads:
            b.readers[key] = ev
        for b in writes:
            b.last_w = {key: ev}
            b.readers = {}
        return ins

    def engines(self):
        return (self.pe, self.act, self.dve, self.pool, self.sp)

    def barrier(self):
        for E in self.engines():
            for E2 in self.engines():
                if E2 is not E and E2.cnt:
                    E.wait(id(E2.sem), E2.sem, E2.cnt)
            if E.own_wait and E.cnt:
                E.wait(id(E.sem), E.sem, E.cnt)
            self.finish(E)

    def finish(self, E):
        for b in self.dma_bufs:
            if b.dcnt:
                E.wait(id(b.dsem), b.dsem, b.dcnt)


class Pool_:
    def __init__(self, nc):
        self.nc = nc
        self.es = ExitStack()

    _uid = [0]

    def t(self, name, shape, dt):
        Pool_._uid[0] += 1
        return self.es.enter_context(self.nc.sbuf_tensor(f"{name}_u{Pool_._uid[0]}", list(shape), dt))

    def close(self):
        self.es.close()


class Ctx:
    pass


I32 = mybir.dt.int32


def rstd_ops(fw, ss, rstd, b_ss, b_rstd, scale_after, eps=EPS, c=None):
    nc = fw.nc
    s2 = scale_after * scale_after
    E = fw.dve
    v, b_v = c.nv, c.b_nv
    t, b_t = c.nt, c.b_nt
    fw.op(E, lambda: nc.vector.tensor_scalar(out=v[:], in0=ss, scalar1=1.0 / (D * s2), scalar2=eps / s2,
                                             op0=ALU.mult, op1=ALU.add),
          reads=[b_ss], writes=[b_v])
    ri = rstd.bitcast(I32)
    fw.op(E, lambda: nc.vector.tensor_scalar(out=ri, in0=v[:].bitcast(I32), scalar1=1, scalar2=None,
                                             op0=ALU.arith_shift_right),
          reads=[b_v], writes=[b_rstd])
    fw.op(E, lambda: nc.vector.tensor_scalar(out=ri, in0=ri, scalar1=-1, scalar2=0x5f3759df,
                                             op0=ALU.mult, op1=ALU.add),
          reads=[b_rstd], writes=[b_rstd])
    for _ in range(3):
        fw.op(E, lambda: nc.vector.scalar_tensor_tensor(out=t[:], in0=rstd, scalar=v[:], in1=rstd,
                                                        op0=ALU.mult, op1=ALU.mult),
              reads=[b_rstd, b_v], writes=[b_t])
        fw.op(E, lambda: nc.vector.tensor_scalar(out=t[:], in0=t[:], scalar1=-0.5, scalar2=1.5,
                                                 op0=ALU.mult, op1=ALU.add),
              reads=[b_t], writes=[b_t])
        fw.op(E, lambda: nc.vector.tensor_tensor(out=rstd, in0=rstd, in1=t[:], op=ALU.mult),
              reads=[b_rstd, b_t], writes=[b_rstd])


def load_weight_bf16(fw, dst_tile, dst_buf, src_ap, nk, chunk=2):
    nc = fw.nc
    v = src_ap.rearrange("(k p) f -> p k f", p=128)
    for k0 in range(0, nk, chunk):
        k1 = min(nk, k0 + chunk)
        fw.dma(fw.pool, dst_tile[:, k0:k1, :], v[:, k0:k1, :], dst_buf, writes=[dst_buf])


def xbuf(c, x_ap, blk_id):
    key = (x_ap.tensor.name, blk_id)
    if key not in c.b_xdram:
        c.b_xdram[key] = Buf(f"xd_{key[0]}_{blk_id}")
    return c.b_xdram[key]


def norm_block(fw, c, x_src, tok0, slot):
    nc = fw.nc
    i = c.nrm_i
    c.nrm_i += 1
    s = i % 2
    xb, b_xb = c.xb[s], c.b_xb[s]
    xn, b_xn = c.xn[slot], c.b_xn[slot]
    ss, b_ss = c.ss[s], c.b_ss[s]
    rs, b_rs = c.rs[s], c.b_rs[s]
    fw.dma(fw.sp, xb[:], x_src[tok0:tok0 + 128, :], b_xb, reads=[xbuf(c, x_src, tok0 // 128)], writes=[b_xb])
    fw.op(fw.act, lambda: nc.scalar.activation(out=xn[:], in_=xb[:], func=AF.Square, accum_out=ss[:]),
          reads=[b_xb], writes=[b_xn, b_ss])
    rstd_ops(fw, ss[:], rs[:], b_ss, b_rs, 1.0, c=c)
    fw.op(fw.act, lambda: nc.scalar.activation(out=xn[:], in_=xb[:], func=AF.Copy, scale=rs[:]),
          reads=[b_xb, b_rs], writes=[b_xn])


def transpose_block(fw, c, slot, blk, hT, b_hT, gT, b_gT):
    nc = fw.nc
    xn, b_xn = c.xn[slot], c.b_xn[slot]
    pa = c.ps[0]
    for half in range(2):
        bp = c.b_ps[half]
        for j in range(4):
            kt = half * 4 + j
            fw.op(fw.pe, lambda: nc.tensor.matmul(pa[:, half * 512 + j * 128: half * 512 + (j + 1) * 128],
                                                  lhsT=xn[:, kt * 128:(kt + 1) * 128], rhs=c.ident[:],
                                                  start=True, stop=True),
                  reads=[b_xn, c.b_ident], writes=[bp], flag=(j == 3))
        fw.op(fw.dve, lambda: nc.vector.tensor_tensor(
            out=hT[:, half * 4:half * 4 + 4, blk * 128:(blk + 1) * 128],
            in0=pa[:, half * 512:(half + 1) * 512].rearrange("p (k t) -> p k t", k=4),
            in1=gT[:, half * 4:half * 4 + 4].unsqueeze(2).to_broadcast([128, 4, 128]),
            op=ALU.mult),
            reads=[bp, b_gT], writes=[b_hT])


def postnorm_residual(fw, c, ps_o, b_ps_o, x_src, x_dst, tok0, gbc, b_gbc, res_scale, eps):
    nc = fw.nc
    i = c.pn_i
    c.pn_i += 1
    s = i % 2
    xr, b_xr = c.xr[s], c.b_xr[s]
    tt, b_tt = c.tt[s], c.b_tt[s]
    ss, b_ss = c.ss2[s], c.b_ss2[s]
    rs, b_rs = c.rs2[s], c.b_rs2[s]
    blk_id = tok0 // 128
    fw.dma(fw.sp, xr[:], x_src[tok0:tok0 + 128, :], b_xr, reads=[xbuf(c, x_src, blk_id)], writes=[b_xr])
    fw.op(fw.act, lambda: nc.scalar.activation(out=c.junk[:], in_=ps_o, func=AF.Square, accum_out=ss[:]),
          reads=list(b_ps_o), writes=[b_ss])
    rstd_ops(fw, ss[:], rs[:], b_ss, b_rs, res_scale, eps=eps, c=c)
    fw.op(fw.dve, lambda: nc.vector.tensor_tensor(out=tt[:], in0=ps_o, in1=gbc[:], op=ALU.mult),
          reads=list(b_ps_o) + [b_gbc], writes=[b_tt])
    fw.op(fw.dve, lambda: nc.vector.scalar_tensor_tensor(out=xr[:], in0=tt[:], scalar=rs[:], in1=xr[:],
                                                         op0=ALU.mult, op1=ALU.add),
          reads=[b_tt, b_rs, b_xr], writes=[b_xr])
    fw.dma(fw.sp, x_dst[tok0:tok0 + 128, :], xr[:], b_xr, reads=[b_xr], writes=[xbuf(c, x_dst, blk_id)])


def alloc_norm_bufs(c, P):
    c.nrm_i = 0
    c.pn_i = 0
    c.nv = P.t("nv", [128, 1], F32)
    c.b_nv = Buf("nv")
    c.nt = P.t("nt", [128, 1], F32)
    c.b_nt = Buf("nt")
    c.xb = [P.t("xb0", [128, D], F32)] * 2
    c.b_xb = [Buf("xb0")] * 2
    c.xn = [P.t(f"xn{s}", [128, D], BF16) for s in range(NB)]
    c.b_xn = [Buf(f"xn{s}") for s in range(NB)]
    c.ss = [P.t(f"ss{s}", [128, 1], F32) for s in range(2)]
    c.b_ss = [Buf(f"ss{s}") for s in range(2)]
    c.rs = [P.t(f"rs{s}", [128, 1], F32) for s in range(2)]
    c.b_rs = [Buf(f"rs{s}") for s in range(2)]


def alloc_post_bufs(c, P):
    c.xr = [P.t(f"xr{s}", [128, D], F32) for s in range(2)]
    c.b_xr = [Buf(f"xr{s}") for s in range(2)]
    c.tt = [P.t("tt0", [128, D], F32)] * 2
    c.b_tt = [Buf("tt0")] * 2
    c.junk = P.t("junk", [128, D], BF16)
    c.ss2 = [P.t(f"ssp{s}", [128, 1], F32) for s in range(2)]
    c.b_ss2 = [Buf(f"ssp{s}") for s in range(2)]
    c.rs2 = [P.t(f"rsp{s}", [128, 1], F32) for s in range(2)]
    c.b_rs2 = [Buf(f"rsp{s}") for s in range(2)]


def ffn_phase(fw, c, x_src, x_dst, wg_d, wu_d, wd_d, gT_d, gbc_d, ntok):
    nc = fw.nc
    P = Pool_(nc)
    wg = P.t("wg", [128, KT, DFF], BF16)
    wu = P.t("wu", [128, KT, DFF], BF16)
    wd = P.t("wd", [128, NF, D], BF16)
    b_wg, b_wu, b_wd = Buf("wg"), Buf("wu"), Buf("wd")
    load_weight_bf16(fw, wg, b_wg, wg_d, KT)
    load_weight_bf16(fw, wu, b_wu, wu_d, KT)
    load_weight_bf16(fw, wd, b_wd, wd_d, NF, chunk=4)
    gT = P.t("gT", [128, KT], F32)
    b_gT = Buf("gT")
    fw.dma(fw.sp, gT[:], gT_d, b_gT, writes=[b_gT])
    gbc = P.t("gbc", [128, D], F32)
    b_gbc = Buf("gbc")
    fw.dma(fw.sp, gbc[:], gbc_d, b_gbc, writes=[b_gbc])
    alloc_norm_bufs(c, P)
    alloc_post_bufs(c, P)
    hT = [P.t(f"hT{s}", [128, KT, T], BF16) for s in range(2)]
    b_hT = [Buf(f"hT{s}") for s in range(2)]
    actT = P.t("actT", [128, NF, T], BF16)
    b_actT = [Buf(f"actT{f}") for f in range(NF)]
    sg = [P.t("sg0", [128, T], F32)] * 2
    b_sg = [Buf("sg0")] * 2
    ps, b_ps = c.ps, c.b_ps
    ntile = ntok // T

    for blk in range(NB):
        norm_block(fw, c, x_src, blk * 128, blk)
    for blk in range(NB):
        transpose_block(fw, c, blk, blk, hT[0], b_hT[0], gT, b_gT)
    gi = 0
    for t in range(ntile):
        h = hT[t % 2]
        bh = b_hT[t % 2]
        nxt = t + 1 < ntile
        for f in range(NF):
            pgu = ps[1 + gi % 2]
            bpg, bpu = b_ps[2 + 2 * (gi % 2)], b_ps[3 + 2 * (gi % 2)]
            s = gi % 2
            gi += 1
            for kt in range(KT):
                fw.op(fw.pe, lambda: nc.tensor.matmul(pgu[:, 0:512], lhsT=wg[:, kt, f * 128:(f + 1) * 128],
                                                      rhs=h[:, kt, :], start=(kt == 0), stop=(kt == KT - 1)),
                      reads=[b_wg, bh], writes=[bpg], flag=(kt == KT - 1))
            for kt in range(KT):
                fw.op(fw.pe, lambda: nc.tensor.matmul(pgu[:, 512:1024], lhsT=wu[:, kt, f * 128:(f + 1) * 128],
                                                      rhs=h[:, kt, :], start=(kt == 0), stop=(kt == KT - 1)),
                      reads=[b_wu, bh], writes=[bpu], flag=(kt == KT - 1))
            fw.op(fw.act, lambda: nc.scalar.activation(out=sg[s][:], in_=pgu[:, 0:512], func=AF.Silu),
                  reads=[bpg], writes=[b_sg[s]])
            fw.op(fw.dve, lambda: nc.vector.tensor_tensor(out=actT[:, f, :], in0=sg[s][:], in1=pgu[:, 512:1024],
                                                          op=ALU.mult),
                  reads=[b_sg[s], bpu], writes=[b_actT[f]])
            if nxt and f in (2, 4, 6, 8):
                blk = (f - 2) // 2
                norm_block(fw, c, x_src, (t + 1) * T + blk * 128, blk)
            if nxt and f in (12, 14, 16, 18):
                blk = (f - 12) // 2
                transpose_block(fw, c, blk, blk, hT[(t + 1) % 2], b_hT[(t + 1) % 2], gT, b_gT)
        for blk in range(NB):
            pi = 3 if blk % 2 == 0 else 0
            po = ps[pi]
            bpo = [b_ps[2 * pi], b_ps[2 * pi + 1]]
            for half in range(2):
                for f in range(NF):
                    fw.op(fw.pe, lambda: nc.tensor.matmul(po[:, half * 512:(half + 1) * 512],
                                                          lhsT=actT[:, f, blk * 128:(blk + 1) * 128],
                                                          rhs=wd[:, f, half * 512:(half + 1) * 512],
                                                          start=(f == 0), stop=(f == NF - 1)),
                          reads=[b_wd, b_actT[f]], writes=[bpo[half]], flag=(f == NF - 1))
            postnorm_residual(fw, c, po[:], bpo, x_src, x_dst, t * T + blk * 128, gbc, b_gbc, 0.5, EPS)
    fw.barrier()
    P.close()


def s5_params(fw, c, P, dr, li):
    nc = fw.nc
    Q = Pool_(nc)
    TWO_PI = 2.0 * math.pi

    def ld(name, src, shape):
        t = Q.t(name, shape, F32)
        b = Buf(name)
        fw.dma(fw.sp, t[:], src, b, writes=[b])
        return t, b
    lr, b_lr = ld("lr", dr["lamre_s"][li], [128, 16])
    lim, b_li = ld("lim", dr["lamim_s"][li], [128, 16])
    ldt, b_ldt = ld("ldt", dr["logdt_s"][li], [128, 16])
    bre, b_bre = ld("bres", dr["bre_s"][li], [128, 16, 16])
    bim, b_bim = ld("bims", dr["bim_s"][li], [128, 16, 16])
    cre, b_cre = ld("cres", dr["cre_s"][li], [128, 16, 16])
    cim, b_cim = ld("cims", dr["cim_s"][li], [128, 16, 16])
    dsk, b_dsk = ld("dsk", dr["d_s"][li], [128, 4])
    iota, b_iota = ld("iota", dr["iota"], [128, 129])
    idf, b_idf = ld("idf", dr["ident_f"], [128, 128])

    V = fw.dve

    def tt(out, a, b, op, r, w):
        fw.op(V, lambda: nc.vector.tensor_tensor(out=out, in0=a, in1=b, op=op), reads=r, writes=w)

    def ts(out, a, s1, s2, op0, op1, r, w):
        if op1 is None:
            fw.op(V, lambda: nc.vector.tensor_scalar(out=out, in0=a, scalar1=s1, scalar2=None, op0=op0), reads=r, writes=w)
        else:
            fw.op(V, lambda: nc.vector.tensor_scalar(out=out, in0=a, scalar1=s1, scalar2=s2, op0=op0, op1=op1),
                  reads=r, writes=w)

    def tmp(name, shape, dt=F32):
        return Q.t(name, shape, dt), Buf(name)
    dt_, b_dt = tmp("dt", [128, 16])
    mag, b_mag = c.rmag, c.b_rmag
    angn, b_angn = tmp("angn", [128, 16])
    fw.op(fw.act, lambda: nc.scalar.activation(out=dt_[:], in_=ldt[:], func=AF.Exp), reads=[b_ldt], writes=[b_dt])
    t16, b_t16 = tmp("t16", [128, 16])
    tt(t16[:], lr[:], dt_[:], ALU.mult, [b_lr, b_dt], [b_t16])
    fw.op(fw.act, lambda: nc.scalar.activation(out=mag[:], in_=t16[:], func=AF.Exp), reads=[b_t16], writes=[b_mag])
    tt(angn[:], lim[:], dt_[:], ALU.mult, [b_li, b_dt], [b_angn])
    ts(angn[:], angn[:], 1.0 / TWO_PI, None, ALU.mult, None, [b_angn], [b_angn])
    ytab, b_ytab = tmp("ytab", [128, 16, 129])
    ki, b_ki = tmp("ki", [128, 16, 129], I32)
    kf, b_kf = tmp("kf", [128, 16, 129])
    for which, tab, b_tab in (("s", c.sintab, c.b_sintab), ("c", c.costab, c.b_costab)):
        tt(ytab[:], angn[:].unsqueeze(2).to_broadcast([128, 16, 129]),
           iota[:].unsqueeze(1).to_broadcast([128, 16, 129]), ALU.mult, [b_angn, b_iota], [b_ytab])
        if which == "c":
            ts(ytab[:], ytab[:], 0.25, None, ALU.add, None, [b_ytab], [b_ytab])
        fw.op(V, lambda: nc.vector.tensor_copy(out=ki[:], in_=ytab[:]), reads=[b_ytab], writes=[b_ki])
        fw.op(V, lambda: nc.vector.tensor_copy(out=kf[:], in_=ki[:]), reads=[b_ki], writes=[b_kf])
        tt(ytab[:], ytab[:], kf[:], ALU.subtract, [b_ytab, b_kf], [b_ytab])
        ts(kf[:], ytab[:], 0.5, None, ALU.is_gt, None, [b_ytab], [b_kf])
        tt(ytab[:], ytab[:], kf[:], ALU.subtract, [b_ytab, b_kf], [b_ytab])
        ts(kf[:], ytab[:], -0.5, None, ALU.is_lt, None, [b_ytab], [b_kf])
        tt(ytab[:], ytab[:], kf[:], ALU.add, [b_ytab, b_kf], [b_ytab])
        fw.op(fw.act, lambda: nc.scalar.activation(out=tab[:], in_=ytab[:], func=AF.Sin, scale=TWO_PI),
              reads=[b_ytab], writes=[b_tab])
    abr, b_abr = tmp("abr", [128, 16])
    abi, b_abi = tmp("abi", [128, 16])
    tt(abr[:], mag[:], c.costab[:, :, 1], ALU.mult, [b_mag, c.b_costab], [b_abr])
    tt(abi[:], mag[:], c.sintab[:, :, 1], ALU.mult, [b_mag, c.b_sintab], [b_abi])
    ts(abr[:], abr[:], -1.0, None, ALU.add, None, [b_abr], [b_abr])
    den, b_den = tmp("den", [128, 16])
    t2, b_t2 = tmp("t2_16", [128, 16])
    tt(den[:], lr[:], lr[:], ALU.mult, [b_lr], [b_den])
    tt(t2[:], lim[:], lim[:], ALU.mult, [b_li], [b_t2])
    tt(den[:], den[:], t2[:], ALU.add, [b_den, b_t2], [b_den])
    fw.op(V, lambda: nc.vector.reciprocal(out=den[:], in_=den[:]), reads=[b_den], writes=[b_den])
    fre, b_fre = tmp("fre", [128, 16])
    fim, b_fim = tmp("fim", [128, 16])
    tt(fre[:], abr[:], lr[:], ALU.mult, [b_abr, b_lr], [b_fre])
    tt(t2[:], abi[:], lim[:], ALU.mult, [b_abi, b_li], [b_t2])
    tt(fre[:], fre[:], t2[:], ALU.add, [b_fre, b_t2], [b_fre])
    tt(fre[:], fre[:], den[:], ALU.mult, [b_fre, b_den], [b_fre])
    tt(fim[:], abi[:], lr[:], ALU.mult, [b_abi, b_lr], [b_fim])
    tt(t2[:], abr[:], lim[:], ALU.mult, [b_abr, b_li], [b_t2])
    tt(fim[:], fim[:], t2[:], ALU.subtract, [b_fim, b_t2], [b_fim])
    tt(fim[:], fim[:], den[:], ALU.mult, [b_fim, b_den], [b_fim])
    bbr, b_bbr = tmp("bbr", [128, 16, 16])
    bbi, b_bbi = tmp("bbi", [128, 16, 16])
    t3, b_t3 = tmp("t3", [128, 16, 16])
    frb = fre[:].unsqueeze(2).to_broadcast([128, 16, 16])
    fib = fim[:].unsqueeze(2).to_broadcast([128, 16, 16])
    tt(bbr[:], bre[:], frb, ALU.mult, [b_bre, b_fre], [b_bbr])
    tt(t3[:], bim[:], fib, ALU.mult, [b_bim, b_fim], [b_t3])
    tt(bbr[:], bbr[:], t3[:], ALU.subtract, [b_bbr, b_t3], [b_bbr])
    tt(bbi[:], bim[:], frb, ALU.mult, [b_bim, b_fre], [b_bbi])
    tt(t3[:], bre[:], fib, ALU.mult, [b_bre, b_fim], [b_t3])
    tt(bbi[:], bbi[:], t3[:], ALU.add, [b_bbi, b_t3], [b_bbi])
    bdt, b_bdt = tmp("bdt", [128, 32, 128])
    ctf, b_ctf = tmp("ctf", [128, 32, 128])
    fw.op(V, lambda: nc.vector.memset(bdt[:], 0.0), writes=[b_bdt])
    fw.op(V, lambda: nc.vector.memset(ctf[:], 0.0), writes=[b_ctf])
    for ci, (sb_, sbuf_b, sc_, scuf_b, csign) in enumerate(((bbr, b_bbr, cre, b_cre, 1.0), (bbi, b_bbi, cim, b_cim, -1.0))):
        for m in range(4):
            for hf in range(2):
                p0, p1 = hf * 64, hf * 64 + 64
                k0 = m * 32 + hf * 16
                dstb = bdt[p0:p1, ci * 16:(ci + 1) * 16, :].rearrange("p (a m) k -> p a m k", m=4)[:, :, m, k0:k0 + 16]
                srcb = sb_[p0:p1, :, :].rearrange("p (a m) h -> p a m h", m=4)[:, :, m, :]
                fw.op(V, lambda: nc.vector.tensor_copy(out=dstb, in_=srcb), reads=[sbuf_b], writes=[b_bdt])
                dstc = ctf[p0:p1, ci * 16:(ci + 1) * 16, :].rearrange("p (a m) k -> p a m k", m=4)[:, :, m, k0:k0 + 16]
                srcc = sc_[p0:p1, :, :].rearrange("p (a m) h -> p a m h", m=4)[:, :, m, :]
                fw.op(V, lambda: nc.vector.tensor_scalar(out=dstc, in0=srcc, scalar1=csign, scalar2=None, op0=ALU.mult),
                      reads=[scuf_b], writes=[b_ctf])
    fw.op(V, lambda: nc.vector.tensor_copy(out=c.CT[:], in_=ctf[:]), reads=[b_ctf], writes=[c.b_CT])
    for g4 in range(8):
        pb = c.ps[g4 % 2 + 1]
        bpb = c.b_ps[2 * (g4 % 2 + 1)]
        for j in range(4):
            ti = g4 * 4 + j
            fw.op(fw.pe, lambda: nc.tensor.matmul(pb[:, j * 128:(j + 1) * 128], lhsT=bdt[:, ti, :], rhs=idf[:],
                                                  start=True, stop=True),
                  reads=[b_bdt, b_idf], writes=[bpb], flag=(j == 3))
        fw.op(V, lambda: nc.vector.tensor_copy(out=c.BD[:, g4 * 4:(g4 + 1) * 4, :],
                                               in_=pb[:, 0:512].rearrange("p (a k) -> p a k", a=4)),
              reads=[bpb], writes=[c.b_BD])
    for kt in range(4):
        fw.op(V, lambda: nc.vector.tensor_scalar(out=c.diagD[:, kt, :], in0=idf[:], scalar1=dsk[:, kt:kt + 1],
                                                 scalar2=None, op0=ALU.mult),
              reads=[b_idf, b_dsk], writes=[c.b_diagD])
    fw.barrier()
    Q.close()


def s_phase(fw, c, dr, li, x_src, glu_d, ntok, seq_len):
    nc = fw.nc
    P = Pool_(nc)
    c.rmag = P.t("rmag", [128, 16], F32); c.b_rmag = Buf("rmag")
    c.costab = P.t("costab", [128, 16, 129], F32); c.b_costab = Buf("costab")
    c.sintab = P.t("sintab", [128, 16, 129], F32); c.b_sintab = Buf("sintab")
    c.BD = P.t("BD", [128, 32, 128], BF16); c.b_BD = Buf("BD")
    c.CT = P.t("CT", [128, 32, 128], BF16); c.b_CT = Buf("CT")
    c.diagD = P.t("diagD", [128, 4, 128], BF16); c.b_diagD = Buf("diagD")
    s5_params(fw, c, P, dr, li)
    wu = P.t("w_u", [128, KT, 512], BF16); b_wu = Buf("w_u")
    fw.dma(fw.pool, wu[:], dr["w_in"][li].rearrange("(k p) f -> p k f", p=128)[:, :, 0:512], b_wu, writes=[b_wu])
    wv = P.t("w_gv", [128, 4, 512], BF16); b_wv = Buf("w_gv")
    wgt = P.t("w_gg", [128, 4, 512], BF16); b_wgt = Buf("w_gg")
    load_weight_bf16(fw, wv, b_wv, dr["w_glu_val"][li], 4, chunk=4)
    load_weight_bf16(fw, wgt, b_wgt, dr["w_glu_gate"][li], 4, chunk=4)
    gT = P.t("gT", [128, KT], F32); b_gT = Buf("gT")
    fw.dma(fw.sp, gT[:], dr["gainsT"][li, 2], b_gT, writes=[b_gT])
    alloc_norm_bufs(c, P)
    hT = [P.t(f"hT{s}", [128, KT, T], BF16) for s in range(2)]
    b_hT = [Buf(f"hT{s}") for s in range(2)]
    uT = P.t("uT", [128, 4, T], BF16); b_uT = [Buf(f"uT{k}") for k in range(4)]
    ygT = P.t("ygT", [128, 4, T], BF16); b_ygT = [Buf(f"ygT{k}") for k in range(4)]

    def wk(name, dt=F32):
        return [P.t(f"{name}{s}", [128, T], dt) for s in range(2)], [Buf(f"{name}{s}") for s in range(2)]
    bre, b_bre = wk("bre"); bim, b_bim = wk("bim")
    t1, b_t1 = wk("t1"); t2, b_t2 = wk("t2")
    btr, b_btr = wk("btr"); bti, b_bti = wk("bti")
    wre, b_wre = wk("wre"); wim, b_wim = wk("wim")
    t5, b_t5 = wk("t5"); t6, b_t6 = wk("t6")
    xre, b_xre = wk("xre", BF16); xim, b_xim = wk("xim", BF16)
    sq, b_sq = wk("sq"); th, b_th = wk("th")
    glo = [P.t(f"glo{s}", [128, T], BF16) for s in range(2)]; b_glo = [Buf(f"glo{s}") for s in range(2)]
    ini_re = P.t("ini_re", [128, 16], F32); ini_im = P.t("ini_im", [128, 16], F32)
    b_ini = [Buf(f"ini{i}") for i in range(16)]
    tn = P.t("tn", [128, 1], F32); b_tn = Buf("tn")
    ps, b_ps = c.ps, c.b_ps
    ntile = ntok // T
    tiles_per_seq = seq_len // T
    V = fw.dve
    G = fw.pool
    pi = 0
    for t in range(ntile):
        tok0 = t * T
        first = (t % tiles_per_seq == 0)
        h, bh = hT[t % 2], b_hT[t % 2]
        for blk in range(NB):
            norm_block(fw, c, x_src, tok0 + blk * 128, blk)
            transpose_block(fw, c, blk, blk, h, bh, gT, b_gT)
        if first:
            fw.op(V, lambda: nc.vector.memset(ini_re[:], 0.0), writes=b_ini)
            fw.op(V, lambda: nc.vector.memset(ini_im[:], 0.0), writes=b_ini)
        for kt in range(4):
            pu, bpu = ps[3][:, (kt % 2) * 512:(kt % 2 + 1) * 512], b_ps[6 + kt % 2]
            for k in range(KT):
                fw.op(fw.pe, lambda: nc.tensor.matmul(pu, lhsT=wu[:, k, kt * 128:(kt + 1) * 128], rhs=h[:, k, :],
                                                      start=(k == 0), stop=(k == KT - 1)),
                      reads=[b_wu, bh], writes=[bpu], flag=(k == KT - 1))
            fw.op(fw.act, lambda: nc.scalar.activation(out=uT[:, kt, :], in_=pu, func=AF.Copy),
                  reads=[bpu], writes=[b_uT[kt]])
        for kt in range(4):
            py, bpy = ps[0][:, (kt % 2) * 512:(kt % 2 + 1) * 512], b_ps[kt % 2]
            for j in range(4):
                i = kt * 4 + j
                s = pi % 2
                pi += 1
                pb = ps[1 + s]
                bpr, bpi_ = b_ps[2 + 2 * s], b_ps[3 + 2 * s]
                fw.op(fw.pe, lambda: nc.tensor.matmul(pb[:, 0:512], lhsT=c.BD[:, i, :], rhs=uT[:, kt, :],
                                                      start=True, stop=True),
                      reads=[c.b_BD, b_uT[kt]], writes=[bpr])
                fw.op(fw.pe, lambda: nc.tensor.matmul(pb[:, 512:1024], lhsT=c.BD[:, 16 + i, :], rhs=uT[:, kt, :],
                                                      start=True, stop=True),
                      reads=[c.b_BD, b_uT[kt]], writes=[bpi_])
                fw.op(fw.act, lambda: nc.scalar.activation(out=bre[s][:], in_=pb[:, 0:512], func=AF.Copy),
                      reads=[bpr], writes=[b_bre[s]])
                fw.op(fw.act, lambda: nc.scalar.activation(out=bim[s][:], in_=pb[:, 512:1024], func=AF.Copy),
                      reads=[bpi_], writes=[b_bim[s]])
                cs = c.costab[:, i, 0:128].unsqueeze(1).to_broadcast([128, 4, 128])
                sn = c.sintab[:, i, 0:128].unsqueeze(1).to_broadcast([128, 4, 128])

                def v3(tl):
                    return tl[:].rearrange("p (s j) -> p s j", j=128)
                fw.op(G, lambda: nc.gpsimd.tensor_tensor(out=v3(t1[s]), in0=v3(bre[s]), in1=cs, op=ALU.mult),
                      reads=[b_bre[s], c.b_costab], writes=[b_t1[s]])
                fw.op(G, lambda: nc.gpsimd.tensor_tensor(out=v3(t2[s]), in0=v3(bim[s]), in1=sn, op=ALU.mult),
                      reads=[b_bim[s], c.b_sintab], writes=[b_t2[s]])
                fw.op(G, lambda: nc.gpsimd.tensor_tensor(out=btr[s][:], in0=t1[s][:], in1=t2[s][:], op=ALU.add),
                      reads=[b_t1[s], b_t2[s]], writes=[b_btr[s]])
                fw.op(G, lambda: nc.gpsimd.tensor_tensor(out=v3(t1[s]), in0=v3(bim[s]), in1=cs, op=ALU.mult),
                      reads=[b_bim[s], c.b_costab], writes=[b_t1[s]])
                fw.op(G, lambda: nc.gpsimd.tensor_tensor(out=v3(t2[s]), in0=v3(bre[s]), in1=sn, op=ALU.mult),
                      reads=[b_bre[s], c.b_sintab], writes=[b_t2[s]])
                fw.op(G, lambda: nc.gpsimd.tensor_tensor(out=bti[s][:], in0=t1[s][:], in1=t2[s][:], op=ALU.subtract),
                      reads=[b_t1[s], b_t2[s]], writes=[b_bti[s]])
                rm = c.rmag[:, i:i + 1].to_broadcast([128, 128])
                for sg_ in range(4):
                    sl = slice(sg_ * 128, (sg_ + 1) * 128)
                    fw.op(V, lambda: nc.vector.tensor_tensor_scan(out=wre[s][:, sl], data0=rm, data1=btr[s][:, sl],
                                                                  initial=ini_re[:, i:i + 1], op0=ALU.mult, op1=ALU.add),
                          reads=[b_btr[s], b_ini[i], c.b_rmag], writes=[b_wre[s]])
                    fw.op(V, lambda: nc.vector.tensor_tensor_scan(out=wim[s][:, sl], data0=rm, data1=bti[s][:, sl],
                                                                  initial=ini_im[:, i:i + 1], op0=ALU.mult, op1=ALU.add),
                          reads=[b_bti[s], b_ini[i], c.b_rmag], writes=[b_wim[s]])
                    last = (sg_ + 1) * 128 - 1
                    er = c.costab[:, i, 128:129]
                    ei = c.sintab[:, i, 128:129]
                    fw.op(V, lambda: nc.vector.tensor_tensor(out=tn[:], in0=wim[s][:, last:last + 1], in1=ei, op=ALU.mult),
                          reads=[b_wim[s], c.b_sintab], writes=[b_tn])
                    fw.op(V, lambda: nc.vector.scalar_tensor_tensor(out=ini_re[:, i:i + 1], in0=wre[s][:, last:last + 1],
                                                                    scalar=er, in1=tn[:], op0=ALU.mult, op1=ALU.subtract),
                          reads=[b_wre[s], b_tn, c.b_costab], writes=[b_ini[i]])
                    fw.op(V, lambda: nc.vector.tensor_tensor(out=tn[:], in0=wre[s][:, last:last + 1], in1=ei, op=ALU.mult),
                          reads=[b_wre[s], c.b_sintab], writes=[b_tn])
                    fw.op(V, lambda: nc.vector.scalar_tensor_tensor(out=ini_im[:, i:i + 1], in0=wim[s][:, last:last + 1],
                                                                    scalar=er, in1=tn[:], op0=ALU.mult, op1=ALU.add),
                          reads=[b_wim[s], b_tn, c.b_costab], writes=[b_ini[i]])
                fw.op(V, lambda: nc.vector.tensor_tensor(out=v3(t5[s]), in0=v3(wre[s]), in1=cs, op=ALU.mult),
                      reads=[b_wre[s], c.b_costab], writes=[b_t5[s]])
                fw.op(V, lambda: nc.vector.tensor_tensor(out=v3(t6[s]), in0=v3(wim[s]), in1=sn, op=ALU.mult),
                      reads=[b_wim[s], c.b_sintab], writes=[b_t6[s]])
                fw.op(V, lambda: nc.vector.tensor_tensor(out=xre[s][:], in0=t5[s][:], in1=t6[s][:], op=ALU.subtract),
                      reads=[b_t5[s], b_t6[s]], writes=[b_xre[s]])
                fw.op(V, lambda: nc.vector.tensor_tensor(out=v3(t5[s]), in0=v3(wim[s]), in1=cs, op=ALU.mult),
                      reads=[b_wim[s], c.b_costab], writes=[b_t5[s]])
                fw.op(V, lambda: nc.vector.tensor_tensor(out=v3(t6[s]), in0=v3(wre[s]), in1=sn, op=ALU.mult),
                      reads=[b_wre[s], c.b_sintab], writes=[b_t6[s]])
                fw.op(V, lambda: nc.vector.tensor_tensor(out=xim[s][:], in0=t5[s][:], in1=t6[s][:], op=ALU.add),
                      reads=[b_t5[s], b_t6[s]], writes=[b_xim[s]])
                fw.op(fw.pe, lambda: nc.tensor.matmul(py, lhsT=c.CT[:, i, :], rhs=xre[s][:], start=(j == 0), stop=False),
                      reads=[c.b_CT, b_xre[s]], writes=[bpy], flag=False)
                fw.op(fw.pe, lambda: nc.tensor.matmul(py, lhsT=c.CT[:, 16 + i, :], rhs=xim[s][:], start=False, stop=False),
                      reads=[c.b_CT, b_xim[s]], writes=[bpy], flag=True)
            fw.op(fw.pe, lambda: nc.tensor.matmul(py, lhsT=c.diagD[:, kt, :], rhs=uT[:, kt, :], start=False, stop=True),
                  reads=[c.b_diagD, b_uT[kt]], writes=[bpy])
            s = kt % 2
            fw.op(fw.act, lambda: nc.scalar.activation(out=sq[s][:], in_=py, func=AF.Square), reads=[bpy], writes=[b_sq[s]])
            fw.op(V, lambda: nc.vector.tensor_scalar(out=sq[s][:], in0=sq[s][:], scalar1=0.044715, scalar2=1.0,
                                                     op0=ALU.mult, op1=ALU.add), reads=[b_sq[s]], writes=[b_sq[s]])
            fw.op(V, lambda: nc.vector.tensor_tensor(out=sq[s][:], in0=sq[s][:], in1=py, op=ALU.mult),
                  reads=[b_sq[s], bpy], writes=[b_sq[s]])
            fw.op(fw.act, lambda: nc.scalar.activation(out=th[s][:], in_=sq[s][:], func=AF.Tanh, scale=0.7978845608028654),
                  reads=[b_sq[s]], writes=[b_th[s]])
            fw.op(V, lambda: nc.vector.scalar_tensor_tensor(out=ygT[:, kt, :], in0=th[s][:], scalar=1.0, in1=py,
                                                            op0=ALU.add, op1=ALU.mult),
                  reads=[b_th[s], bpy], writes=[b_ygT[kt]])
        for oc in range(4):
            pv, bpv = ps[3][:, 0:512], b_ps[6]
            pg, bpg = ps[3][:, 512:1024], b_ps[7]
            for k in range(4):
                fw.op(fw.pe, lambda: nc.tensor.matmul(pv, lhsT=wv[:, k, oc * 128:(oc + 1) * 128], rhs=ygT[:, k, :],
                                                      start=(k == 0), stop=(k == 3)),
                      reads=[b_wv, b_ygT[k]], writes=[bpv], flag=(k == 3))
            for k in range(4):
                fw.op(fw.pe, lambda: nc.tensor.matmul(pg, lhsT=wgt[:, k, oc * 128:(oc + 1) * 128], rhs=ygT[:, k, :],
                                                      start=(k == 0), stop=(k == 3)),
                      reads=[b_wgt, b_ygT[k]], writes=[bpg], flag=(k == 3))
            s = oc % 2
            fw.op(fw.act, lambda: nc.scalar.activation(out=th[s][:], in_=pg, func=AF.Tanh, scale=0.25),
                  reads=[bpg], writes=[b_th[s]])
            fw.op(V, lambda: nc.vector.scalar_tensor_tensor(out=glo[s][:], in0=th[s][:], scalar=1.0, in1=pv,
                                                            op0=ALU.add, op1=ALU.mult),
                  reads=[b_th[s], bpv], writes=[b_glo[s]])
            fw.dma(fw.sp, glu_d[oc * 128:(oc + 1) * 128, tok0:tok0 + T], glo[s][:], b_glo[s], reads=[b_glo[s]],
                   writes=[c.b_glud[t][oc]])
    fw.barrier()
    P.close()


def a_phase(fw, c, dr, li, x_src, x_dst, glu_d, ntok, seq_len):
    nc = fw.nc
    P = Pool_(nc)
    NQ = 3584
    w = P.t("w_qkvg", [128, KT, NQ], BF16); b_w = Buf("w_qkvg")
    wv_ = dr["w_in"][li].rearrange("(k p) f -> p k f", p=128)
    for k0 in range(0, KT, 2):
        fw.dma(fw.pool, w[:, k0:k0 + 2, :], wv_[:, k0:k0 + 2, 512:4096], b_w, writes=[b_w])
    wos = P.t("w_os", [128, 4, D], BF16); b_wos = Buf("w_os")
    woa = P.t("w_oa", [128, 4, D], BF16); b_woa = Buf("w_oa")
    wo = P.t("w_o", [128, KT, D], BF16); b_wo = Buf("w_o")
    load_weight_bf16(fw, wos, b_wos, dr["w_out_ssm"][li], 4, chunk=4)
    load_weight_bf16(fw, woa, b_woa, dr["w_out_att"][li], 4, chunk=4)
    load_weight_bf16(fw, wo, b_wo, dr["w_o"][li], KT, chunk=4)
    gT = P.t("gT", [128, KT], F32); b_gT = Buf("gT")
    fw.dma(fw.sp, gT[:], dr["gainsT"][li, 2], b_gT, writes=[b_gT])
    gbc = P.t("gbc", [128, D], F32); b_gbc = Buf("gbc")
    fw.dma(fw.sp, gbc[:], dr["gains_bc"][li, 3], b_gbc, writes=[b_gbc])
    R8 = P.t("R8", [128, 8, 641], BF16); b_R8 = Buf("R8")
    Bm = P.t("Bm", [128, 8, 2, 128], BF16); b_Bm = Buf("Bm")
    Q = Pool_(nc)
    rf = Q.t("rf", [128, 8, 641], F32); b_rf = Buf("rf")
    for h0 in range(0, 8, 2):
        fw.dma(fw.sp, rf[:, h0:h0 + 2, :], dr["biasT"][li, h0:h0 + 2].rearrange("h k f -> k h f"), b_rf, writes=[b_rf])
    fw.op(fw.dve, lambda: nc.vector.tensor_scalar(out=R8[:], in0=rf[:], scalar1=8.0, scalar2=None, op0=ALU.mult),
          reads=[b_rf], writes=[b_R8])
    fw.op(fw.dve, lambda: nc.vector.tensor_copy(out=Bm[:, :, 0, :], in_=R8[:, :, 1:129]), reads=[b_R8], writes=[b_Bm])
    fw.op(fw.dve, lambda: nc.vector.tensor_copy(out=Bm[:, :, 1, :], in_=R8[:, :, 513:641]), reads=[b_R8], writes=[b_Bm])
    fw.op(fw.dve, lambda: nc.vector.memset(Bm[64:128, :, 0, 0:64], -240000.0), writes=[b_Bm])
    fw.op(fw.dve, lambda: nc.vector.memset(Bm[0:64, :, 1, 64:128], -240000.0), writes=[b_Bm])
    fw.barrier()
    Q.close()
    ones = P.t("ones", [128, 64], BF16); b_ones = Buf("ones")
    fw.op(fw.dve, lambda: nc.vector.memset(ones[:], 1.0), writes=[b_ones])
    kT = P.t("kTr", [128, 4, 1024], BF16)
    b_kT = [Buf(f"kT{s}") for s in range(2)]
    Vr = P.t("Vr", [128, 8, 512], BF16)
    b_Vr = [Buf(f"Vr{s}") for s in range(8)]
    alloc_norm_bufs(c, P)
    alloc_post_bufs(c, P)
    hT = P.t("hT", [128, KT, T], BF16); b_hT = Buf("hT")
    qT = P.t("qT", [128, 4, T], BF16); b_qT = [Buf(f"qT{k}") for k in range(4)]
    attT = P.t("attT", [128, 4, T], BF16); b_attT = [Buf(f"attT{k}") for k in range(4)]
    gluT = P.t("gluT", [128, 4, T], BF16); b_gluT = Buf("gluT")
    PT = [P.t(f"PT{s}", [128, 640], BF16) for s in range(2)]; b_PT = [Buf(f"PT{s}") for s in range(2)]
    rden = [P.t(f"rden{s}", [128, 128], F32) for s in range(2)]; b_rden = [Buf(f"rden{s}") for s in range(2)]
    ta = P.t("ta", [128, T], F32); b_ta = Buf("ta")
    tb = P.t("tb", [128, T], F32); b_tb = Buf("tb")
    m1 = P.t("m1", [128, T], F32); b_m1 = Buf("m1")
    m2 = P.t("m2", [128, T], F32); b_m2 = Buf("m2")
    mrg = P.t("mrg", [128, KT, T], BF16); b_mrg = [Buf(f"mrg{k}") for k in range(KT)]
    ps, b_ps = c.ps, c.b_ps
    ntile = ntok // T
    tps = seq_len // T
    V = fw.dve
    si = 0
    ndi = 0
    for t in range(ntile):
        tok0 = t * T
        tl = t % tps
        half = tl % 2
        for blk in range(NB):
            norm_block(fw, c, x_src, tok0 + blk * 128, blk)
            transpose_block(fw, c, blk, blk, hT, b_hT, gT, b_gT)
        fw.dma(fw.sp, gluT[:], glu_d[:, tok0:tok0 + T].rearrange("(k p) t -> p k t", p=128), b_gluT,
               reads=c.b_glud[t], writes=[b_gluT])
        for hp in range(4):
            for which in range(2):
                pq, bpq = ps[0][:, which * 512:(which + 1) * 512], b_ps[which]
                c0 = which * 512 + hp * 128
                for k in range(KT):
                    fw.op(fw.pe, lambda: nc.tensor.matmul(pq, lhsT=w[:, k, c0:c0 + 128], rhs=hT[:, k, :],
                                                          start=(k == 0), stop=(k == KT - 1)),
                          reads=[b_w, b_hT], writes=[bpq], flag=(k == KT - 1))
                if which == 0:
                    fw.op(fw.act, lambda: nc.scalar.activation(out=qT[:, hp, :], in_=pq, func=AF.Copy),
                          reads=[bpq], writes=[b_qT[hp]])
                else:
                    fw.op(fw.act, lambda: nc.scalar.activation(out=kT[:, hp, half * 512:(half + 1) * 512], in_=pq,
                                                               func=AF.Copy),
                          reads=[bpq], writes=[b_kT[half]])
        for blk in range(NB):
            pv, bpv = ps[3][:, (blk % 2) * 512:(blk % 2 + 1) * 512], b_ps[6 + blk % 2]
            for k in range(KT):
                fw.op(fw.pe, lambda: nc.tensor.matmul(pv, lhsT=hT[:, k, blk * 128:(blk + 1) * 128], rhs=w[:, k, 1024:1536],
                                                      start=(k == 0), stop=(k == KT - 1)),
                      reads=[b_w, b_hT], writes=[bpv], flag=(k == KT - 1))
            slot = half * 4 + blk
            fw.op(fw.act, lambda: nc.scalar.activation(out=Vr[:, slot, :], in_=pv, func=AF.Copy),
                  reads=[bpv], writes=[b_Vr[slot]])
        for blk in range(NB):
            jb = tl * 4 + blk
            deltas = [d_ for d_ in range(5) if jb - d_ >= 0]
            nd = len(deltas)
            for hp in range(4):
                pnd, bpnd = ps[3][:, (ndi % 2) * 512:(ndi % 2) * 512 + 256], b_ps[6 + ndi % 2]
                ndi += 1
                for e in range(2):
                    hd = 2 * hp + e
                    r0 = e * 64
                    s = si % 2
                    si += 1
                    pS = ps[1 + s]
                    bS = [b_ps[2 + 2 * s], b_ps[3 + 2 * s]]
                    for di, d_ in enumerate(deltas):
                        kb = jb - d_
                        kc = (kb % 8) * 128
                        o = pS[:, di * 128:(di + 1) * 128]
                        bo = bS[0] if di < 4 else bS[1]
                        fw.op(fw.pe, lambda: nc.tensor.matmul(o, lhsT=kT[r0:r0 + 64, hp, kc:kc + 128],
                                                              rhs=qT[r0:r0 + 64, hp, blk * 128:(blk + 1) * 128],
                                                              start=True, stop=False),
                              reads=[b_kT[(kb % 8) // 4], b_qT[hp]], writes=[bo], flag=False)
                        if d_ == 0:
                            bt = Bm[:, hd, 0, :]
                        elif d_ == 4:
                            bt = Bm[:, hd, 1, :]
                        else:
                            bt = R8[:, hd, 1 + 128 * d_:129 + 128 * d_]
                        fw.op(fw.pe, lambda: nc.tensor.matmul(o, lhsT=c.ident[:], rhs=bt, start=False, stop=True),
                              reads=[b_R8, b_Bm, c.b_ident], writes=[bo], flag=(di == nd - 1 or di == 3))
                    fw.op(fw.act, lambda: nc.scalar.activation(out=PT[s][:, 0:nd * 128], in_=pS[:, 0:nd * 128],
                                                               func=AF.Exp, scale=0.125),
                          reads=bS, writes=[b_PT[s]])
                    for di, d_ in enumerate(deltas):
                        kb = jb - d_
                        fw.op(fw.pe, lambda: nc.tensor.matmul(pnd[r0:r0 + 64, 0:128],
                                                              lhsT=Vr[:, kb % 8, hd * 64:(hd + 1) * 64],
                                                              rhs=PT[s][:, di * 128:(di + 1) * 128],
                                                              start=(di == 0), stop=(di == nd - 1)),
                              reads=[b_Vr[kb % 8], b_PT[s]], writes=[bpnd], flag=False)
                    for di, d_ in enumerate(deltas):
                        fw.op(fw.pe, lambda: nc.tensor.matmul(pnd[r0:r0 + 64, 128:256], lhsT=ones[:],
                                                              rhs=PT[s][:, di * 128:(di + 1) * 128],
                                                              start=(di == 0), stop=(di == nd - 1)),
                              reads=[b_ones, b_PT[s]], writes=[bpnd], flag=(di == nd - 1))
                rs_ = hp % 2
                fw.op(V, lambda: nc.vector.reciprocal(out=rden[rs_][:], in_=pnd[:, 128:256]),
                      reads=[bpnd], writes=[b_rden[rs_]])
                fw.op(V, lambda: nc.vector.tensor_tensor(out=attT[:, hp, blk * 128:(blk + 1) * 128], in0=pnd[:, 0:128],
                                                         in1=rden[rs_][:], op=ALU.mult),
                      reads=[bpnd, b_rden[rs_]], writes=[b_attT[hp]])
        for cc in range(KT):
            pga, bpga = ps[1][:, 0:512], b_ps[2]
            pgb, bpgb = ps[1][:, 512:1024], b_ps[3]
            pya, bpya = ps[2][:, 0:512], b_ps[4]
            pyb, bpyb = ps[2][:, 512:1024], b_ps[5]
            for k in range(KT):
                fw.op(fw.pe, lambda: nc.tensor.matmul(pga, lhsT=w[:, k, 1536 + cc * 128:1536 + (cc + 1) * 128],
                                                      rhs=hT[:, k, :], start=(k == 0), stop=(k == KT - 1)),
                      reads=[b_w, b_hT], writes=[bpga], flag=(k == KT - 1))
            for k in range(KT):
                fw.op(fw.pe, lambda: nc.tensor.matmul(pgb, lhsT=w[:, k, 2560 + cc * 128:2560 + (cc + 1) * 128],
                                                      rhs=hT[:, k, :], start=(k == 0), stop=(k == KT - 1)),
                      reads=[b_w, b_hT], writes=[bpgb], flag=(k == KT - 1))
            for k in range(4):
                fw.op(fw.pe, lambda: nc.tensor.matmul(pya, lhsT=wos[:, k, cc * 128:(cc + 1) * 128], rhs=gluT[:, k, :],
                                                      start=(k == 0), stop=(k == 3)),
                      reads=[b_wos, b_gluT], writes=[bpya], flag=(k == 3))
            for k in range(4):
                fw.op(fw.pe, lambda: nc.tensor.matmul(pyb, lhsT=woa[:, k, cc * 128:(cc + 1) * 128], rhs=attT[:, k, :],
                                                      start=(k == 0), stop=(k == 3)),
                      reads=[b_woa, b_attT[k]], writes=[bpyb], flag=(k == 3))
            fw.op(fw.act, lambda: nc.scalar.activation(out=ta[:], in_=pga, func=AF.Tanh, scale=0.5),
                  reads=[bpga], writes=[b_ta])
            fw.op(fw.act, lambda: nc.scalar.activation(out=tb[:], in_=pgb, func=AF.Tanh, scale=0.5),
                  reads=[bpgb], writes=[b_tb])
            fw.op(V, lambda: nc.vector.scalar_tensor_tensor(out=m1[:], in0=ta[:], scalar=1.0, in1=pya,
                                                            op0=ALU.add, op1=ALU.mult),
                  reads=[b_ta, bpya], writes=[b_m1])
            fw.op(V, lambda: nc.vector.scalar_tensor_tensor(out=m2[:], in0=tb[:], scalar=1.0, in1=pyb,
                                                            op0=ALU.add, op1=ALU.mult),
                  reads=[b_tb, bpyb], writes=[b_m2])
            fw.op(V, lambda: nc.vector.scalar_tensor_tensor(out=mrg[:, cc, :], in0=m1[:], scalar=0.25, in1=m2[:],
                                                            op0=ALU.mult, op1=ALU.add),
                  reads=[b_m1, b_m2], writes=[b_mrg[cc]])
        for blk in range(NB):
            pi_ = 3 if blk % 2 == 0 else 0
            po = ps[pi_]
            bpo = [b_ps[2 * pi_], b_ps[2 * pi_ + 1]]
            for hf in range(2):
                for cc in range(KT):
                    fw.op(fw.pe, lambda: nc.tensor.matmul(po[:, hf * 512:(hf + 1) * 512],
                                                          lhsT=mrg[:, cc, blk * 128:(blk + 1) * 128],
                                                          rhs=wo[:, cc, hf * 512:(hf + 1) * 512],
                                                          start=(cc == 0), stop=(cc == KT - 1)),
                          reads=[b_wo, b_mrg[cc]], writes=[bpo[hf]], flag=(cc == KT - 1))
            postnorm_residual(fw, c, po[:], bpo, x_src, x_dst, tok0 + blk * 128, gbc, b_gbc, 1.0, 4.0 * EPS)
    fw.barrier()
    P.close()


def build_program(ntok, layers, phases=("F1", "S", "A", "F2"), seq_len=4096):
    nc = bass.Bass("TRN2", target_bir_lowering=False)
    L = len(layers)
    dr = {}

    def din(name, shape, dt=F32):
        dr[name] = nc.dram_tensor(name, list(shape), dt, kind="ExternalInput").ap()
        return dr[name]

    x_in = din("x", [ntok, D])
    din("ident", [128, 128], BF16)
    din("ffn_w_gate", [L, 2, D, DFF])
    din("ffn_w_up", [L, 2, D, DFF])
    din("ffn_w_down", [L, 2, DFF, D])
    din("gainsT", [L, 6, 128, KT])
    din("w_in", [L, D, 4096])
    din("w_glu_val", [L, 512, 512])
    din("w_glu_gate", [L, 512, 512])
    for nm in ("lamre_s", "lamim_s", "logdt_s"):
        din(nm, [L, 128, 16])
    for nm in ("bre_s", "bim_s", "cre_s", "cim_s"):
        din(nm, [L, 128, 16, 16])
    din("d_s", [L, 128, 4])
    din("w_out_ssm", [L, 512, D])
    din("w_out_att", [L, 512, D])
    din("w_o", [L, D, D])
    din("biasT", [L, 8, 128, 641])
    din("iota", [128, 129])
    din("ident_f", [128, 128])
    dbg = os.environ.get("K_DEBUG_GLU")
    glu_d = nc.dram_tensor("glu_d", [512, ntok], BF16, kind=("ExternalOutput" if dbg else "Internal")).ap()
    din("gains_bc", [L, 6, 128, D])
    y = nc.dram_tensor("y", [ntok, D], F32, kind="ExternalOutput").ap()

    with ExitStack() as es:
        fw = FW(nc, es)
        c = Ctx()
        c.b_xdram = {}
        c.ps = [es.enter_context(nc.psum_tensor(f"ps{i}", [128, 1024], F32)) for i in range(4)]
        c.b_ps = [Buf(f"psb{i}") for i in range(8)]
        c.ident = es.enter_context(nc.sbuf_tensor("ident_sb", [128, 128], BF16))
        c.b_ident = Buf("ident")
        fw.dma(fw.sp, c.ident[:], dr["ident"], c.b_ident, writes=[c.b_ident])
        cur = x_in
        for li in range(L):
            for ph in phases:
                if ph in ("F1", "F2"):
                    i = 0 if ph == "F1" else 1
                    ffn_phase(fw, c, cur, y, dr["ffn_w_gate"][li, i], dr["ffn_w_up"][li, i],
                              dr["ffn_w_down"][li, i], dr["gainsT"][li, 4 * i], dr["gains_bc"][li, 4 * i + 1], ntok)
                    cur = y
                elif ph == "S":
                    c.b_glud = [[Buf(f"glud{t}_{o}") for o in range(4)] for t in range(ntok // T)]
                    s_phase(fw, c, dr, li, cur, glu_d, ntok, seq_len)
                elif ph == "A":
                    a_phase(fw, c, dr, li, cur, y, glu_d, ntok, seq_len)
                    cur = y
        fw.finish(fw.sp)
        print(f"[build] instructions={fw.ninst} waits={fw.nwaits} sems={fw.nsem}")
    return nc


def host_inputs(inputs, layers):
    import ml_dtypes
    g = np.asarray(inputs["norm_gains"], np.float32)[layers]
    L = len(layers)
    out = {
        "ident": np.eye(128, dtype=np.float32).astype(ml_dtypes.bfloat16),
        "ffn_w_gate": np.ascontiguousarray(inputs["ffn_w_gate"][layers]),
        "ffn_w_up": np.ascontiguousarray(inputs["ffn_w_up"][layers]),
        "ffn_w_down": np.ascontiguousarray(inputs["ffn_w_down"][layers]),
        "gainsT": np.ascontiguousarray(g.reshape(L, 6, KT, 128).transpose(0, 1, 3, 2)),
        "gains_bc": np.ascontiguousarray(np.broadcast_to(g[:, :, None, :], (L, 6, 128, D))),
    }
    G_, P_, H_ = 32, 64, 16

    def st(a):
        a = np.asarray(a, np.float32)[layers]
        return np.ascontiguousarray(a.reshape(L, 16, 2, P_).transpose(0, 2, 3, 1).reshape(L, 128, 16))
    out["lamre_s"] = st(inputs["lam_re"])
    out["lamim_s"] = st(inputs["lam_im"])
    out["logdt_s"] = st(np.broadcast_to(np.asarray(inputs["log_dt"], np.float32)[:, :, None], (len(inputs["log_dt"]), G_, P_)))
    for nm, key in (("bre_s", "b_re"), ("bim_s", "b_im")):
        a = np.asarray(inputs[key], np.float32)[layers]
        out[nm] = np.ascontiguousarray(a.reshape(L, 16, 2, P_, H_).transpose(0, 2, 3, 1, 4).reshape(L, 128, 16, H_))
    for nm, key in (("cre_s", "c_re"), ("cim_s", "c_im")):
        a = np.asarray(inputs[key], np.float32)[layers]
        out[nm] = np.ascontiguousarray(a.reshape(L, 16, 2, H_, P_).transpose(0, 2, 4, 1, 3).reshape(L, 128, 16, H_))
    d = np.asarray(inputs["d_skip"], np.float32)[layers]
    out["d_s"] = np.ascontiguousarray(d.reshape(L, 4, 128).transpose(0, 2, 1))
    out["iota"] = np.ascontiguousarray(np.broadcast_to(np.arange(129, dtype=np.float32)[None, :], (128, 129)))
    out["ident_f"] = np.eye(128, dtype=np.float32)
    rb = np.asarray(inputs["rel_bias"], np.float32)[layers]
    rb_ext = np.concatenate([rb, np.repeat(rb[:, :, -1:], 768 - 257, axis=2)], axis=2)
    idx = (127 - np.arange(128))[:, None] + np.arange(641)[None, :]
    out["biasT"] = np.ascontiguousarray(rb_ext[:, :, idx])
    for k in ("w_in", "w_glu_val", "w_glu_gate", "w_out_ssm", "w_out_att", "w_o"):
        out[k] = np.ascontiguousarray(np.asarray(inputs[k], np.float32)[layers])
    return out


_PROG_CACHE = {}
LAYERS_PER_LAUNCH = 1


def _get_program(ntok, nlayers, seq_len):
    key = (ntok, nlayers, seq_len)
    if key not in _PROG_CACHE:
        _PROG_CACHE[key] = build_program(ntok, list(range(nlayers)), seq_len=seq_len)
    return _PROG_CACHE[key]


def kernel(**inputs):
    x = np.ascontiguousarray(np.asarray(inputs["x"], np.float32))
    B, S, _ = x.shape
    per = B // N_CORES
    ntok = per * S
    depth = int(np.asarray(inputs["norm_gains"]).shape[0])
    lpl = LAYERS_PER_LAUNCH
    shards = [x[i * per:(i + 1) * per].reshape(ntok, D) for i in range(N_CORES)]
    for l0 in range(0, depth, lpl):
        layers = list(range(l0, l0 + lpl))
        nc = _get_program(ntok, lpl, S)
        shared = host_inputs(inputs, layers)
        in_maps = []
        for i in range(N_CORES):
            m = dict(shared)
            m["x"] = shards[i]
            in_maps.append(m)
        res = run_bass_kernel_spmd(nc, in_maps, core_ids=list(range(N_CORES)))
        shards = [np.asarray(res.results[i]["y"], np.float32) for i in range(N_CORES)]
    out = np.stack([s.reshape(per, S, D) for s in shards], axis=0).reshape(B, S, D)
    return out.astype(np.float32)
```

```python
from contextlib import ExitStack
import math
import os
import numpy as np
import concourse.bass as bass
import concourse.mybir as mybir
from concourse.bass_utils import run_bass_kernel_spmd

F32 = mybir.dt.float32
BF16 = mybir.dt.bfloat16
AF = mybir.ActivationFunctionType
ALU = mybir.AluOpType
AX = mybir.AxisListType

D = 1024
DFF = 2816
NF = DFF // 128
KT = D // 128
T = 512
NB = T // 128
EPS = 1e-6
N_CORES = 8


class Buf:
    __slots__ = ("name", "last_w", "readers", "dsem", "dcnt")

    def __init__(self, name):
        self.name = name
        self.last_w = {}
        self.readers = {}
        self.dsem = None
        self.dcnt = 0


class Eng:
    def __init__(self, fw, eng, name, own_wait):
        self.fw = fw
        self.eng = eng
        self.name = name
        self.sem = fw.new_sem("e_" + name)
        self.cnt = 0
        self.seen = {}
        self.own_wait = own_wait
        self.pend_r = []
        self.pend_w = []

    def wait(self, key, sem, val):
        if sem is self.sem and not self.own_wait:
            return
        if self.seen.get(key, 0) < val:
            self.eng.wait_ge(sem, val)
            self.seen[key] = val
            self.fw.nwaits += 1


class FW:
    def __init__(self, nc, es):
        self.nc = nc
        self.es = es
        self.nsem = 0
        self.nwaits = 0
        self.ninst = 0
        self.pe = Eng(self, nc.tensor, "pe", False)
        self.act = Eng(self, nc.scalar, "act", True)
        self.dve = Eng(self, nc.vector, "dve", True)
        self.pool = Eng(self, nc.gpsimd, "pool", True)
        self.sp = Eng(self, nc.sync, "sp", False)
        self.dma_bufs = []

    def new_sem(self, name):
        self.nsem += 1
        return self.es.enter_context(self.nc.semaphore(f"{name}_{self.nsem}"))

    def _deps(self, E, reads, writes):
        for b in reads:
            for k, (s, v) in b.last_w.items():
                E.wait(k, s, v)
        for b in writes:
            for k, (s, v) in b.last_w.items():
                E.wait(k, s, v)
            for k, (s, v) in b.readers.items():
                E.wait(k, s, v)

    def op(self, E, fn, reads=(), writes=(), flag=True):
        self._deps(E, reads, writes)
        ins = fn()
        self.ninst += 1
        if flag:
            E.cnt += 1
            ins.then_inc(E.sem, 1)
            key = id(E.sem)
            ev = (E.sem, E.cnt)
            for b in E.pend_r:
                b.readers[key] = ev
            for b in reads:
                b.readers[key] = ev
            for b in E.pend_w:
                b.last_w = {key: ev}
                b.readers = {}
            for b in writes:
                b.last_w = {key: ev}
                b.readers = {}
            E.pend_r = []
            E.pend_w = []
        else:
            E.pend_r.extend(reads)
            E.pend_w.extend(writes)
        return ins

    def dma(self, Q, out, in_, sb, reads=(), writes=()):
        self._deps(Q, reads, writes)
        if sb.dsem is None:
            sb.dsem = self.new_sem("d_" + sb.name)
            self.dma_bufs.append(sb)
        ins = Q.eng.dma_start(out=out, in_=in_)
        sb.dcnt += 16
        ins.then_inc(sb.dsem, 16)
        self.ninst += 1
        key = id(sb.dsem)
        ev = (sb.dsem, sb.dcnt)
        for b in reads:
            b.readers[key] = ev
        for b in writes:
            b.last_w = {key: ev}
            b.readers = {}
        return ins

    def engines(self):
        return (self.pe, self.act, self.dve, self.pool, self.sp)

    def barrier(self):
        for E in self.engines():
            for E2 in self.engines():
                if E2 is not E and E2.cnt:
                    E.wait(id(E2.sem), E2.sem, E2.cnt)
            if E.own_wait and E.cnt:
                E.wait(id(E.sem), E.sem, E.cnt)
            self.finish(E)

    def finish(self, E):
        for b in self.dma_bufs:
            if b.dcnt:
                E.wait(id(b.dsem), b.dsem, b.dcnt)


class Pool_:
    def __init__(self, nc):
        self.nc = nc
        self.es = ExitStack()

    _uid = [0]

    def t(self, name, shape, dt):
        Pool_._uid[0] += 1
        return self.es.enter_context(self.nc.sbuf_tensor(f"{name}_u{Pool_._uid[0]}", list(shape), dt))

    def close(self):
        self.es.close()


class Ctx:
    pass


I32 = mybir.dt.int32


def rstd_ops(fw, ss, rstd, b_ss, b_rstd, scale_after, eps=EPS, c=None):
    nc = fw.nc
    s2 = scale_after * scale_after
    E = fw.dve
    v, b_v = c.nv, c.b_nv
    t, b_t = c.nt, c.b_nt
    fw.op(E, lambda: nc.vector.tensor_scalar(out=v[:], in0=ss, scalar1=1.0 / (D * s2), scalar2=eps / s2,
                                             op0=ALU.mult, op1=ALU.add),
          reads=[b_ss], writes=[b_v])
    ri = rstd.bitcast(I32)
    fw.op(E, lambda: nc.vector.tensor_scalar(out=ri, in0=v[:].bitcast(I32), scalar1=1, scalar2=None,
                                             op0=ALU.arith_shift_right),
          reads=[b_v], writes=[b_rstd])
    fw.op(E, lambda: nc.vector.tensor_scalar(out=ri, in0=ri, scalar1=-1, scalar2=0x5f3759df,
                                             op0=ALU.mult, op1=ALU.add),
          reads=[b_rstd], writes=[b_rstd])
    for _ in range(3):
        fw.op(E, lambda: nc.vector.scalar_tensor_tensor(out=t[:], in0=rstd, scalar=v[:], in1=rstd,
                                                        op0=ALU.mult, op1=ALU.mult),
              reads=[b_rstd, b_v], writes=[b_t])
        fw.op(E, lambda: nc.vector.tensor_scalar(out=t[:], in0=t[:], scalar1=-0.5, scalar2=1.5,
                                                 op0=ALU.mult, op1=ALU.add),
              reads=[b_t], writes=[b_t])
        fw.op(E, lambda: nc.vector.tensor_tensor(out=rstd, in0=rstd, in1=t[:], op=ALU.mult),
              reads=[b_rstd, b_t], writes=[b_rstd])


def load_weight_bf16(fw, dst_tile, dst_buf, src_ap, nk, chunk=2):
    nc = fw.nc
    v = src_ap.rearrange("(k p) f -> p k f", p=128)
    for k0 in range(0, nk, chunk):
        k1 = min(nk, k0 + chunk)
        fw.dma(fw.pool, dst_tile[:, k0:k1, :], v[:, k0:k1, :], dst_buf, writes=[dst_buf])


def xbuf(c, x_ap, blk_id):
    key = (x_ap.tensor.name, blk_id)
    if key not in c.b_xdram:
        c.b_xdram[key] = Buf(f"xd_{key[0]}_{blk_id}")
    return c.b_xdram[key]


def norm_block(fw, c, x_src, tok0, slot):
    nc = fw.nc
    i = c.nrm_i
    c.nrm_i += 1
    s = i % 2
    xb, b_xb = c.xb[s], c.b_xb[s]
    xn, b_xn = c.xn[slot], c.b_xn[slot]
    ss, b_ss = c.ss[s], c.b_ss[s]
    rs, b_rs = c.rs[s], c.b_rs[s]
    fw.dma(fw.sp, xb[:], x_src[tok0:tok0 + 128, :], b_xb, reads=[xbuf(c, x_src, tok0 // 128)], writes=[b_xb])
    fw.op(fw.act, lambda: nc.scalar.activation(out=xn[:], in_=xb[:], func=AF.Square, accum_out=ss[:]),
          reads=[b_xb], writes=[b_xn, b_ss])
    rstd_ops(fw, ss[:], rs[:], b_ss, b_rs, 1.0, c=c)
    fw.op(fw.act, lambda: nc.scalar.activation(out=xn[:], in_=xb[:], func=AF.Copy, scale=rs[:]),
          reads=[b_xb, b_rs], writes=[b_xn])


def transpose_block(fw, c, slot, blk, hT, b_hT, gT, b_gT):
    nc = fw.nc
    xn, b_xn = c.xn[slot], c.b_xn[slot]
    pa = c.ps[0]
    for half in range(2):
        bp = c.b_ps[half]
        for j in range(4):
            kt = half * 4 + j
            fw.op(fw.pe, lambda: nc.tensor.matmul(pa[:, half * 512 + j * 128: half * 512 + (j + 1) * 128],
                                                  lhsT=xn[:, kt * 128:(kt + 1) * 128], rhs=c.ident[:],
                                                  start=True, stop=True),
                  reads=[b_xn, c.b_ident], writes=[bp], flag=(j == 3))
        fw.op(fw.dve, lambda: nc.vector.tensor_tensor(
            out=hT[:, half * 4:half * 4 + 4, blk * 128:(blk + 1) * 128],
            in0=pa[:, half * 512:(half + 1) * 512].rearrange("p (k t) -> p k t", k=4),
            in1=gT[:, half * 4:half * 4 + 4].unsqueeze(2).to_broadcast([128, 4, 128]),
            op=ALU.mult),
            reads=[bp, b_gT], writes=[b_hT])


def postnorm_residual(fw, c, ps_o, b_ps_o, x_src, x_dst, tok0, gbc, b_gbc, res_scale, eps):
    nc = fw.nc
    i = c.pn_i
    c.pn_i += 1
    s = i % 2
    xr, b_xr = c.xr[s], c.b_xr[s]
    tt, b_tt = c.tt[s], c.b_tt[s]
    ss, b_ss = c.ss2[s], c.b_ss2[s]
    rs, b_rs = c.rs2[s], c.b_rs2[s]
    blk_id = tok0 // 128
    fw.dma(fw.sp, xr[:], x_src[tok0:tok0 + 128, :], b_xr, reads=[xbuf(c, x_src, blk_id)], writes=[b_xr])
    fw.op(fw.act, lambda: nc.scalar.activation(out=c.junk[:], in_=ps_o, func=AF.Square, accum_out=ss[:]),
          reads=list(b_ps_o), writes=[b_ss])
    rstd_ops(fw, ss[:], rs[:], b_ss, b_rs, res_scale, eps=eps, c=c)
    fw.op(fw.dve, lambda: nc.vector.tensor_tensor(out=tt[:], in0=ps_o, in1=gbc[:], op=ALU.mult),
          reads=list(b_ps_o) + [b_gbc], writes=[b_tt])
    fw.op(fw.dve, lambda: nc.vector.scalar_tensor_tensor(out=xr[:], in0=tt[:], scalar=rs[:], in1=xr[:],
                                                         op0=ALU.mult, op1=ALU.add),
          reads=[b_tt, b_rs, b_xr], writes=[b_xr])
    fw.dma(fw.sp, x_dst[tok0:tok0 + 128, :], xr[:], b_xr, reads=[b_xr], writes=[xbuf(c, x_dst, blk_id)])


def alloc_norm_bufs(c, P):
    c.nrm_i = 0
    c.pn_i = 0
    c.nv = P.t("nv", [128, 1], F32)
    c.b_nv = Buf("nv")
    c.nt = P.t("nt", [128, 1], F32)
    c.b_nt = Buf("nt")
    c.xb = [P.t("xb0", [128, D], F32)] * 2
    c.b_xb = [Buf("xb0")] * 2
    c.xn = [P.t(f"xn{s}", [128, D], BF16) for s in range(NB)]
    c.b_xn = [Buf(f"xn{s}") for s in range(NB)]
    c.ss = [P.t(f"ss{s}", [128, 1], F32) for s in range(2)]
    c.b_ss = [Buf(f"ss{s}") for s in range(2)]
    c.rs = [P.t(f"rs{s}", [128, 1], F32) for s in range(2)]
    c.b_rs = [Buf(f"rs{s}") for s in range(2)]


def alloc_post_bufs(c, P):
    c.xr = [P.t(f"xr{s}", [128, D], F32) for s in range(2)]
    c.b_xr = [Buf(f"xr{s}") for s in range(2)]
    c.tt = [P.t("tt0", [128, D], F32)] * 2
    c.b_tt = [Buf("tt0")] * 2
    c.junk = P.t("junk", [128, D], BF16)
    c.ss2 = [P.t(f"ssp{s}", [128, 1], F32) for s in range(2)]
    c.b_ss2 = [Buf(f"ssp{s}") for s in range(2)]
    c.rs2 = [P.t(f"rsp{s}", [128, 1], F32) for s in range(2)]
    c.b_rs2 = [Buf(f"rsp{s}") for s in range(2)]


def ffn_phase(fw, c, x_src, x_dst, wg_d, wu_d, wd_d, gT_d, gbc_d, ntok):
    nc = fw.nc
    P = Pool_(nc)
    wg = P.t("wg", [128, KT, DFF], BF16)
    wu = P.t("wu", [128, KT, DFF], BF16)
    wd = P.t("wd", [128, NF, D], BF16)
    b_wg, b_wu, b_wd = Buf("wg"), Buf("wu"), Buf("wd")
    load_weight_bf16(fw, wg, b_wg, wg_d, KT)
    load_weight_bf16(fw, wu, b_wu, wu_d, KT)
    load_weight_bf16(fw, wd, b_wd, wd_d, NF, chunk=4)
    gT = P.t("gT", [128, KT], F32)
    b_gT = Buf("gT")
    fw.dma(fw.sp, gT[:], gT_d, b_gT, writes=[b_gT])
    gbc = P.t("gbc", [128, D], F32)
    b_gbc = Buf("gbc")
    fw.dma(fw.sp, gbc[:], gbc_d, b_gbc, writes=[b_gbc])
    alloc_norm_bufs(c, P)
    alloc_post_bufs(c, P)
    hT = [P.t(f"hT{s}", [128, KT, T], BF16) for s in range(2)]
    b_hT = [Buf(f"hT{s}") for s in range(2)]
    actT = P.t("actT", [128, NF, T], BF16)
    b_actT = [Buf(f"actT{f}") for f in range(NF)]
    sg = [P.t("sg0", [128, T], F32)] * 2
    b_sg = [Buf("sg0")] * 2
    ps, b_ps = c.ps, c.b_ps
    ntile = ntok // T

    for blk in range(NB):
        norm_block(fw, c, x_src, blk * 128, blk)
    for blk in range(NB):
        transpose_block(fw, c, blk, blk, hT[0], b_hT[0], gT, b_gT)
    gi = 0
    for t in range(ntile):
        h = hT[t % 2]
        bh = b_hT[t % 2]
        nxt = t + 1 < ntile
        for f in range(NF):
            pgu = ps[1 + gi % 2]
            bpg, bpu = b_ps[2 + 2 * (gi % 2)], b_ps[3 + 2 * (gi % 2)]
            s = gi % 2
            gi += 1
            for kt in range(KT):
                fw.op(fw.pe, lambda: nc.tensor.matmul(pgu[:, 0:512], lhsT=wg[:, kt, f * 128:(f + 1) * 128],
                                                      rhs=h[:, kt, :], start=(kt == 0), stop=(kt == KT - 1)),
                      reads=[b_wg, bh], writes=[bpg], flag=(kt == KT - 1))
            for kt in range(KT):
                fw.op(fw.pe, lambda: nc.tensor.matmul(pgu[:, 512:1024], lhsT=wu[:, kt, f * 128:(f + 1) * 128],
                                                      rhs=h[:, kt, :], start=(kt == 0), stop=(kt == KT - 1)),
                      reads=[b_wu, bh], writes=[bpu], flag=(kt == KT - 1))
            fw.op(fw.act, lambda: nc.scalar.activation(out=sg[s][:], in_=pgu[:, 0:512], func=AF.Silu),
                  reads=[bpg], writes=[b_sg[s]])
            fw.op(fw.dve, lambda: nc.vector.tensor_tensor(out=actT[:, f, :], in0=sg[s][:], in1=pgu[:, 512:1024],
                                                          op=ALU.mult),
                  reads=[b_sg[s], bpu], writes=[b_actT[f]])
            if nxt and f in (2, 4, 6, 8):
                blk = (f - 2) // 2
                norm_block(fw, c, x_src, (t + 1) * T + blk * 128, blk)
            if nxt and f in (12, 14, 16, 18):
                blk = (f - 12) // 2
                transpose_block(fw, c, blk, blk, hT[(t + 1) % 2], b_hT[(t + 1) % 2], gT, b_gT)
        for blk in range(NB):
            pi = 3 if blk % 2 == 0 else 0
            po = ps[pi]
            bpo = [b_ps[2 * pi], b_ps[2 * pi + 1]]
            for half in range(2):
                for f in range(NF):
                    fw.op(fw.pe, lambda: nc.tensor.matmul(po[:, half * 512:(half + 1) * 512],
                                                          lhsT=actT[:, f, blk * 128:(blk + 1) * 128],
                                                          rhs=wd[:, f, half * 512:(half + 1) * 512],
                                                          start=(f == 0), stop=(f == NF - 1)),
                          reads=[b_wd, b_actT[f]], writes=[bpo[half]], flag=(f == NF - 1))
            postnorm_residual(fw, c, po[:], bpo, x_src, x_dst, t * T + blk * 128, gbc, b_gbc, 0.5, EPS)
    fw.barrier()
    P.close()


def s5_params(fw, c, P, dr, li):
    nc = fw.nc
    Q = Pool_(nc)
    TWO_PI = 2.0 * math.pi

    def ld(name, src, shape):
        t = Q.t(name, shape, F32)
        b = Buf(name)
        fw.dma(fw.sp, t[:], src, b, writes=[b])
        return t, b
    lr, b_lr = ld("lr", dr["lamre_s"][li], [128, 16])
    lim, b_li = ld("lim", dr["lamim_s"][li], [128, 16])
    ldt, b_ldt = ld("ldt", dr["logdt_s"][li], [128, 16])
    bre, b_bre = ld("bres", dr["bre_s"][li], [128, 16, 16])
    bim, b_bim = ld("bims", dr["bim_s"][li], [128, 16, 16])
    cre, b_cre = ld("cres", dr["cre_s"][li], [128, 16, 16])
    cim, b_cim = ld("cims", dr["cim_s"][li], [128, 16, 16])
    dsk, b_dsk = ld("dsk", dr["d_s"][li], [128, 4])
    iota, b_iota = ld("iota", dr["iota"], [128, 129])
    idf, b_idf = ld("idf", dr["ident_f"], [128, 128])

    V = fw.dve

    def tt(out, a, b, op, r, w):
        fw.op(V, lambda: nc.vector.tensor_tensor(out=out, in0=a, in1=b, op=op), reads=r, writes=w)

    def ts(out, a, s1, s2, op0, op1, r, w):
        if op1 is None:
            fw.op(V, lambda: nc.vector.tensor_scalar(out=out, in0=a, scalar1=s1, scalar2=None, op0=op0), reads=r, writes=w)
        else:
            fw.op(V, lambda: nc.vector.tensor_scalar(out=out, in0=a, scalar1=s1, scalar2=s2, op0=op0, op1=op1),
                  reads=r, writes=w)

    def tmp(name, shape, dt=F32):
        return Q.t(name, shape, dt), Buf(name)
    dt_, b_dt = tmp("dt", [128, 16])
    mag, b_mag = c.rmag, c.b_rmag
    angn, b_angn = tmp("angn", [128, 16])
    fw.op(fw.act, lambda: nc.scalar.activation(out=dt_[:], in_=ldt[:], func=AF.Exp), reads=[b_ldt], writes=[b_dt])
    t16, b_t16 = tmp("t16", [128, 16])
    tt(t16[:], lr[:], dt_[:], ALU.mult, [b_lr, b_dt], [b_t16])
    fw.op(fw.act, lambda: nc.scalar.activation(out=mag[:], in_=t16[:], func=AF.Exp), reads=[b_t16], writes=[b_mag])
    tt(angn[:], lim[:], dt_[:], ALU.mult, [b_li, b_dt], [b_angn])
    ts(angn[:], angn[:], 1.0 / TWO_PI, None, ALU.mult, None, [b_angn], [b_angn])
    ytab, b_ytab = tmp("ytab", [128, 16, 129])
    ki, b_ki = tmp("ki", [128, 16, 129], I32)
    kf, b_kf = tmp("kf", [128, 16, 129])
    for which, tab, b_tab in (("s", c.sintab, c.b_sintab), ("c", c.costab, c.b_costab)):
        tt(ytab[:], angn[:].unsqueeze(2).to_broadcast([128, 16, 129]),
           iota[:].unsqueeze(1).to_broadcast([128, 16, 129]), ALU.mult, [b_angn, b_iota], [b_ytab])
        if which == "c":
            ts(ytab[:], ytab[:], 0.25, None, ALU.add, None, [b_ytab], [b_ytab])
        fw.op(V, lambda: nc.vector.tensor_copy(out=ki[:], in_=ytab[:]), reads=[b_ytab], writes=[b_ki])
        fw.op(V, lambda: nc.vector.tensor_copy(out=kf[:], in_=ki[:]), reads=[b_ki], writes=[b_kf])
        tt(ytab[:], ytab[:], kf[:], ALU.subtract, [b_ytab, b_kf], [b_ytab])
        ts(kf[:], ytab[:], 0.5, None, ALU.is_gt, None, [b_ytab], [b_kf])
        tt(ytab[:], ytab[:], kf[:], ALU.subtract, [b_ytab, b_kf], [b_ytab])
        ts(kf[:], ytab[:], -0.5, None, ALU.is_lt, None, [b_ytab], [b_kf])
        tt(ytab[:], ytab[:], kf[:], ALU.add, [b_ytab, b_kf], [b_ytab])
        fw.op(fw.act, lambda: nc.scalar.activation(out=tab[:], in_=ytab[:], func=AF.Sin, scale=TWO_PI),
              reads=[b_ytab], writes=[b_tab])
    abr, b_abr = tmp("abr", [128, 16])
    abi, b_abi = tmp("abi", [128, 16])
    tt(abr[:], mag[:], c.costab[:, :, 1], ALU.mult, [b_mag, c.b_costab], [b_abr])
    tt(abi[:], mag[:], c.sintab[:, :, 1], ALU.mult, [b_mag, c.b_sintab], [b_abi])
    ts(abr[:], abr[:], -1.0, None, ALU.add, None, [b_abr], [b_abr])
    den, b_den = tmp("den", [128, 16])
    t2, b_t2 = tmp("t2_16", [128, 16])
    tt(den[:], lr[:], lr[:], ALU.mult, [b_lr], [b_den])
    tt(t2[:], lim[:], lim[:], ALU.mult, [b_li], [b_t2])
    tt(den[:], den[:], t2[:], ALU.add, [b_den, b_t2], [b_den])
    fw.op(V, lambda: nc.vector.reciprocal(out=den[:], in_=den[:]), reads=[b_den], writes=[b_den])
    fre, b_fre = tmp("fre", [128, 16])
    fim, b_fim = tmp("fim", [128, 16])
    tt(fre[:], abr[:], lr[:], ALU.mult, [b_abr, b_lr], [b_fre])
    tt(t2[:], abi[:], lim[:], ALU.mult, [b_abi, b_li], [b_t2])
    tt(fre[:], fre[:], t2[:], ALU.add, [b_fre, b_t2], [b_fre])
    tt(fre[:], fre[:], den[:], ALU.mult, [b_fre, b_den], [b_fre])
    tt(fim[:], abi[:], lr[:], ALU.mult, [b_abi, b_lr], [b_fim])
    tt(t2[:], abr[:], lim[:], ALU.mult, [b_abr, b_li], [b_t2])
    tt(fim[:], fim[:], t2[:], ALU.subtract, [b_fim, b_t2], [b_fim])
    tt(fim[:], fim[:], den[:], ALU.mult, [b_fim, b_den], [b_fim])
    bbr, b_bbr = tmp("bbr", [128, 16, 16])
    bbi, b_bbi = tmp("bbi", [128, 16, 16])
    t3, b_t3 = tmp("t3", [128, 16, 16])
    frb = fre[:].unsqueeze(2).to_broadcast([128, 16, 16])
    fib = fim[:].unsqueeze(2).to_broadcast([128, 16, 16])
    tt(bbr[:], bre[:], frb, ALU.mult, [b_bre, b_fre], [b_bbr])
    tt(t3[:], bim[:], fib, ALU.mult, [b_bim, b_fim], [b_t3])
    tt(bbr[:], bbr[:], t3[:], ALU.subtract, [b_bbr, b_t3], [b_bbr])
    tt(bbi[:], bim[:], frb, ALU.mult, [b_bim, b_fre], [b_bbi])
    tt(t3[:], bre[:], fib, ALU.mult, [b_bre, b_fim], [b_t3])
    tt(bbi[:], bbi[:], t3[:], ALU.add, [b_bbi, b_t3], [b_bbi])
    bdt, b_bdt = tmp("bdt", [128, 32, 128])
    ctf, b_ctf = tmp("ctf", [128, 32, 128])
    fw.op(V, lambda: nc.vector.memset(bdt[:], 0.0), writes=[b_bdt])
    fw.op(V, lambda: nc.vector.memset(ctf[:], 0.0), writes=[b_ctf])
    for ci, (sb_, sbuf_b, sc_, scuf_b, csign) in enumerate(((bbr, b_bbr, cre, b_cre, 1.0), (bbi, b_bbi, cim, b_cim, -1.0))):
        for m in range(4):
            for hf in range(2):
                p0, p1 = hf * 64, hf * 64 + 64
                k0 = m * 32 + hf * 16
                dstb = bdt[p0:p1, ci * 16:(ci + 1) * 16, :].rearrange("p (a m) k -> p a m k", m=4)[:, :, m, k0:k0 + 16]
                srcb = sb_[p0:p1, :, :].rearrange("p (a m) h -> p a m h", m=4)[:, :, m, :]
                fw.op(V, lambda: nc.vector.tensor_copy(out=dstb, in_=srcb), reads=[sbuf_b], writes=[b_bdt])
                dstc = ctf[p0:p1, ci * 16:(ci + 1) * 16, :].rearrange("p (a m) k -> p a m k", m=4)[:, :, m, k0:k0 + 16]
                srcc = sc_[p0:p1, :, :].rearrange("p (a m) h -> p a m h", m=4)[:, :, m, :]
                fw.op(V, lambda: nc.vector.tensor_scalar(out=dstc, in0=srcc, scalar1=csign, scalar2=None, op0=ALU.mult),
                      reads=[scuf_b], writes=[b_ctf])
    fw.op(V, lambda: nc.vector.tensor_copy(out=c.CT[:], in_=ctf[:]), reads=[b_ctf], writes=[c.b_CT])
    for g4 in range(8):
        pb = c.ps[g4 % 2 + 1]
        bpb = c.b_ps[2 * (g4 % 2 + 1)]
        for j in range(4):
            ti = g4 * 4 + j
            fw.op(fw.pe, lambda: nc.tensor.matmul(pb[:, j * 128:(j + 1) * 128], lhsT=bdt[:, ti, :], rhs=idf[:],
                                                  start=True, stop=True),
                  reads=[b_bdt, b_idf], writes=[bpb], flag=(j == 3))
        fw.op(V, lambda: nc.vector.tensor_copy(out=c.BD[:, g4 * 4:(g4 + 1) * 4, :],
                                               in_=pb[:, 0:512].rearrange("p (a k) -> p a k", a=4)),
              reads=[bpb], writes=[c.b_BD])
    for kt in range(4):
        fw.op(V, lambda: nc.vector.tensor_scalar(out=c.diagD[:, kt, :], in0=idf[:], scalar1=dsk[:, kt:kt + 1],
                                                 scalar2=None, op0=ALU.mult),
              reads=[b_idf, b_dsk], writes=[c.b_diagD])
    fw.barrier()
    Q.close()


def s_phase(fw, c, dr, li, x_src, glu_d, ntok, seq_len):
    nc = fw.nc
    P = Pool_(nc)
    c.rmag = P.t("rmag", [128, 16], F32); c.b_rmag = Buf("rmag")
    c.costab = P.t("costab", [128, 16, 129], F32); c.b_costab = Buf("costab")
    c.sintab = P.t("sintab", [128, 16, 129], F32); c.b_sintab = Buf("sintab")
    c.BD = P.t("BD", [128, 32, 128], BF16); c.b_BD = Buf("BD")
    c.CT = P.t("CT", [128, 32, 128], BF16); c.b_CT = Buf("CT")
    c.diagD = P.t("diagD", [128, 4, 128], BF16); c.b_diagD = Buf("diagD")
    s5_params(fw, c, P, dr, li)
    wu = P.t("w_u", [128, KT, 512], BF16); b_wu = Buf("w_u")
    fw.dma(fw.pool, wu[:], dr["w_in"][li].rearrange("(k p) f -> p k f", p=128)[:, :, 0:512], b_wu, writes=[b_wu])
    wv = P.t("w_gv", [128, 4, 512], BF16); b_wv = Buf("w_gv")
    wgt = P.t("w_gg", [128, 4, 512], BF16); b_wgt = Buf("w_gg")
    load_weight_bf16(fw, wv, b_wv, dr["w_glu_val"][li], 4, chunk=4)
    load_weight_bf16(fw, wgt, b_wgt, dr["w_glu_gate"][li], 4, chunk=4)
    gT = P.t("gT", [128, KT], F32); b_gT = Buf("gT")
    fw.dma(fw.sp, gT[:], dr["gainsT"][li, 2], b_gT, writes=[b_gT])
    alloc_norm_bufs(c, P)
    hT = [P.t(f"hT{s}", [128, KT, T], BF16) for s in range(2)]
    b_hT = [Buf(f"hT{s}") for s in range(2)]
    uT = P.t("uT", [128, 4, T], BF16); b_uT = [Buf(f"uT{k}") for k in range(4)]
    ygT = P.t("ygT", [128, 4, T], BF16); b_ygT = [Buf(f"ygT{k}") for k in range(4)]

    NS = 4

    def wk(name, dt=F32):
        return [P.t(f"{name}{s}", [128, T], dt) for s in range(NS)], [Buf(f"{name}{s}") for s in range(NS)]
    bre, b_bre = wk("bre"); bim, b_bim = wk("bim")
    t1, b_t1 = wk("t1"); t2, b_t2 = wk("t2")
    btr, b_btr = wk("btr"); bti, b_bti = wk("bti")
    wre, b_wre = wk("wre"); wim, b_wim = wk("wim")
    t5, b_t5 = wk("t5"); t6, b_t6 = wk("t6")
    xre, b_xre = wk("xre", BF16); xim, b_xim = wk("xim", BF16)
    sq, b_sq = wk("sq"); th, b_th = wk("th")
    glo = [P.t(f"glo{s}", [128, T], BF16) for s in range(2)]; b_glo = [Buf(f"glo{s}") for s in range(2)]
    ini_re = P.t("ini_re", [128, 16], F32); ini_im = P.t("ini_im", [128, 16], F32)
    b_ini = [Buf(f"ini{i}") for i in range(16)]
    tn = [P.t(f"tn{s}", [128, 2], F32) for s in range(NS)]
    b_tn = [[Buf(f"tn{s}a"), Buf(f"tn{s}b")] for s in range(NS)]
    ps, b_ps = c.ps, c.b_ps
    ntile = ntok // T
    tiles_per_seq = seq_len // T
    V = fw.dve
    G = fw.pool
    pi = 0
    for t in range(ntile):
        tok0 = t * T
        first = (t % tiles_per_seq == 0)
        h, bh = hT[t % 2], b_hT[t % 2]
        for blk in range(NB):
            norm_block(fw, c, x_src, tok0 + blk * 128, blk)
            transpose_block(fw, c, blk, blk, h, bh, gT, b_gT)
        if first:
            fw.op(V, lambda: nc.vector.memset(ini_re[:], 0.0), writes=b_ini)
            fw.op(V, lambda: nc.vector.memset(ini_im[:], 0.0), writes=b_ini)
        for kt in range(4):
            pu, bpu = ps[3][:, (kt % 2) * 512:(kt % 2 + 1) * 512], b_ps[6 + kt % 2]
            for k in range(KT):
                fw.op(fw.pe, lambda: nc.tensor.matmul(pu, lhsT=wu[:, k, kt * 128:(kt + 1) * 128], rhs=h[:, k, :],
                                                      start=(k == 0), stop=(k == KT - 1)),
                      reads=[b_wu, bh], writes=[bpu], flag=(k == KT - 1))
            fw.op(fw.act, lambda: nc.scalar.activation(out=uT[:, kt, :], in_=pu, func=AF.Copy),
                  reads=[bpu], writes=[b_uT[kt]])
        for kt in range(4):
            py, bpy = ps[0][:, (kt % 2) * 512:(kt % 2 + 1) * 512], b_ps[kt % 2]
            def v3(tl):
                return tl[:].rearrange("p (s j) -> p s j", j=128)

            def stage1(i, s):
                pb = ps[1 + s % 2]
                bpr, bpi_ = b_ps[2 + 2 * (s % 2)], b_ps[3 + 2 * (s % 2)]
                fw.op(fw.pe, lambda: nc.tensor.matmul(pb[:, 0:512], lhsT=c.BD[:, i, :], rhs=uT[:, kt, :],
                                                      start=True, stop=True),
                      reads=[c.b_BD, b_uT[kt]], writes=[bpr])
                fw.op(fw.pe, lambda: nc.tensor.matmul(pb[:, 512:1024], lhsT=c.BD[:, 16 + i, :], rhs=uT[:, kt, :],
                                                      start=True, stop=True),
                      reads=[c.b_BD, b_uT[kt]], writes=[bpi_])
                fw.op(fw.act, lambda: nc.scalar.activation(out=bre[s][:], in_=pb[:, 0:512], func=AF.Copy),
                      reads=[bpr], writes=[b_bre[s]])
                fw.op(fw.act, lambda: nc.scalar.activation(out=bim[s][:], in_=pb[:, 512:1024], func=AF.Copy),
                      reads=[bpi_], writes=[b_bim[s]])
                cs = c.costab[:, i, 0:128].unsqueeze(1).to_broadcast([128, 4, 128])
                sn = c.sintab[:, i, 0:128].unsqueeze(1).to_broadcast([128, 4, 128])
                fw.op(G, lambda: nc.gpsimd.tensor_tensor(out=v3(t1[s]), in0=v3(bre[s]), in1=cs, op=ALU.mult),
                      reads=[b_bre[s], c.b_costab], writes=[b_t1[s]])
                fw.op(G, lambda: nc.gpsimd.tensor_tensor(out=v3(t2[s]), in0=v3(bim[s]), in1=sn, op=ALU.mult),
                      reads=[b_bim[s], c.b_sintab], writes=[b_t2[s]])
                fw.op(G, lambda: nc.gpsimd.tensor_tensor(out=btr[s][:], in0=t1[s][:], in1=t2[s][:], op=ALU.add),
                      reads=[b_t1[s], b_t2[s]], writes=[b_btr[s]])
                fw.op(G, lambda: nc.gpsimd.tensor_tensor(out=v3(t1[s]), in0=v3(bim[s]), in1=cs, op=ALU.mult),
                      reads=[b_bim[s], c.b_costab], writes=[b_t1[s]])
                fw.op(G, lambda: nc.gpsimd.tensor_tensor(out=v3(t2[s]), in0=v3(bre[s]), in1=sn, op=ALU.mult),
                      reads=[b_bre[s], c.b_sintab], writes=[b_t2[s]])
                fw.op(G, lambda: nc.gpsimd.tensor_tensor(out=bti[s][:], in0=t1[s][:], in1=t2[s][:], op=ALU.subtract),
                      reads=[b_t1[s], b_t2[s]], writes=[b_bti[s]])

            def scan_seg(i, s, sg_):
                rm = c.rmag[:, i:i + 1].to_broadcast([128, 128])
                sl = slice(sg_ * 128, (sg_ + 1) * 128)
                fw.op(V, lambda: nc.vector.tensor_tensor_scan(out=wre[s][:, sl], data0=rm, data1=btr[s][:, sl],
                                                              initial=ini_re[:, i:i + 1], op0=ALU.mult, op1=ALU.add),
                      reads=[b_btr[s], b_ini[i], c.b_rmag], writes=[b_wre[s]])
                fw.op(V, lambda: nc.vector.tensor_tensor_scan(out=wim[s][:, sl], data0=rm, data1=bti[s][:, sl],
                                                              initial=ini_im[:, i:i + 1], op0=ALU.mult, op1=ALU.add),
                      reads=[b_bti[s], b_ini[i], c.b_rmag], writes=[b_wim[s]])

            def handoff(i, s, sg_, step):
                last = (sg_ + 1) * 128 - 1
                er = c.costab[:, i, 128:129]
                ei = c.sintab[:, i, 128:129]
                tn_, b_tn_ = tn[s], b_tn[s]
                if step == 0:
                    fw.op(V, lambda: nc.vector.tensor_tensor(out=tn_[:, 0:1], in0=wim[s][:, last:last + 1], in1=ei, op=ALU.mult),
                          reads=[b_wim[s], c.b_sintab], writes=[b_tn_[0]])
                    fw.op(V, lambda: nc.vector.tensor_tensor(out=tn_[:, 1:2], in0=wre[s][:, last:last + 1], in1=ei, op=ALU.mult),
                          reads=[b_wre[s], c.b_sintab], writes=[b_tn_[1]])
                else:
                    fw.op(V, lambda: nc.vector.scalar_tensor_tensor(out=ini_re[:, i:i + 1], in0=wre[s][:, last:last + 1],
                                                                    scalar=er, in1=tn_[:, 0:1], op0=ALU.mult, op1=ALU.subtract),
                          reads=[b_wre[s], b_tn_[0], c.b_costab], writes=[b_ini[i]])
                    fw.op(V, lambda: nc.vector.scalar_tensor_tensor(out=ini_im[:, i:i + 1], in0=wim[s][:, last:last + 1],
                                                                    scalar=er, in1=tn_[:, 1:2], op0=ALU.mult, op1=ALU.add),
                          reads=[b_wim[s], b_tn_[1], c.b_costab], writes=[b_ini[i]])

            def stage3(i, s, j):
                cs = c.costab[:, i, 0:128].unsqueeze(1).to_broadcast([128, 4, 128])
                sn = c.sintab[:, i, 0:128].unsqueeze(1).to_broadcast([128, 4, 128])
                fw.op(V, lambda: nc.vector.tensor_tensor(out=v3(t5[s]), in0=v3(wre[s]), in1=cs, op=ALU.mult),
                      reads=[b_wre[s], c.b_costab], writes=[b_t5[s]])
                fw.op(V, lambda: nc.vector.tensor_tensor(out=v3(t6[s]), in0=v3(wim[s]), in1=sn, op=ALU.mult),
                      reads=[b_wim[s], c.b_sintab], writes=[b_t6[s]])
                fw.op(V, lambda: nc.vector.tensor_tensor(out=xre[s][:], in0=t5[s][:], in1=t6[s][:], op=ALU.subtract),
                      reads=[b_t5[s], b_t6[s]], writes=[b_xre[s]])
                fw.op(V, lambda: nc.vector.tensor_tensor(out=v3(t5[s]), in0=v3(wim[s]), in1=cs, op=ALU.mult),
                      reads=[b_wim[s], c.b_costab], writes=[b_t5[s]])
                fw.op(V, lambda: nc.vector.tensor_tensor(out=v3(t6[s]), in0=v3(wre[s]), in1=sn, op=ALU.mult),
                      reads=[b_wre[s], c.b_sintab], writes=[b_t6[s]])
                fw.op(V, lambda: nc.vector.tensor_tensor(out=xim[s][:], in0=t5[s][:], in1=t6[s][:], op=ALU.add),
                      reads=[b_t5[s], b_t6[s]], writes=[b_xim[s]])
                fw.op(fw.pe, lambda: nc.tensor.matmul(py, lhsT=c.CT[:, i, :], rhs=xre[s][:], start=(j == 0), stop=False),
                      reads=[c.b_CT, b_xre[s]], writes=[bpy], flag=False)
                fw.op(fw.pe, lambda: nc.tensor.matmul(py, lhsT=c.CT[:, 16 + i, :], rhs=xim[s][:], start=False, stop=False),
                      reads=[c.b_CT, b_xim[s]], writes=[bpy], flag=True)

            for j0 in (0,):
                prs = [(kt * 4 + jj, jj, jj) for jj in range(4)]
                for (i, s, j) in prs:
                    stage1(i, s)
                for sg_ in range(4):
                    for (i, s, j) in prs:
                        scan_seg(i, s, sg_)
                    for step in range(2):
                        for (i, s, j) in prs:
                            handoff(i, s, sg_, step)
                for (i, s, j) in prs:
                    stage3(i, s, j)
            fw.op(fw.pe, lambda: nc.tensor.matmul(py, lhsT=c.diagD[:, kt, :], rhs=uT[:, kt, :], start=False, stop=True),
                  reads=[c.b_diagD, b_uT[kt]], writes=[bpy])
            s = kt % 2
            fw.op(fw.act, lambda: nc.scalar.activation(out=sq[s][:], in_=py, func=AF.Square), reads=[bpy], writes=[b_sq[s]])
            fw.op(V, lambda: nc.vector.tensor_scalar(out=sq[s][:], in0=sq[s][:], scalar1=0.044715, scalar2=1.0,
                                                     op0=ALU.mult, op1=ALU.add), reads=[b_sq[s]], writes=[b_sq[s]])
            fw.op(V, lambda: nc.vector.tensor_tensor(out=sq[s][:], in0=sq[s][:], in1=py, op=ALU.mult),
                  reads=[b_sq[s], bpy], writes=[b_sq[s]])
            fw.op(fw.act, lambda: nc.scalar.activation(out=th[s][:], in_=sq[s][:], func=AF.Tanh, scale=0.7978845608028654),
                  reads=[b_sq[s]], writes=[b_th[s]])
            fw.op(V, lambda: nc.vector.scalar_tensor_tensor(out=ygT[:, kt, :], in0=th[s][:], scalar=1.0, in1=py,
                                                            op0=ALU.add, op1=ALU.mult),
                  reads=[b_th[s], bpy], writes=[b_ygT[kt]])
        for oc in range(4):
            pv, bpv = ps[3][:, 0:512], b_ps[6]
            pg, bpg = ps[3][:, 512:1024], b_ps[7]
            for k in range(4):
                fw.op(fw.pe, lambda: nc.tensor.matmul(pv, lhsT=wv[:, k, oc * 128:(oc + 1) * 128], rhs=ygT[:, k, :],
                                                      start=(k == 0), stop=(k == 3)),
                      reads=[b_wv, b_ygT[k]], writes=[bpv], flag=(k == 3))
            for k in range(4):
                fw.op(fw.pe, lambda: nc.tensor.matmul(pg, lhsT=wgt[:, k, oc * 128:(oc + 1) * 128], rhs=ygT[:, k, :],
                                                      start=(k == 0), stop=(k == 3)),
                      reads=[b_wgt, b_ygT[k]], writes=[bpg], flag=(k == 3))
            s = oc % 2
            fw.op(fw.act, lambda: nc.scalar.activation(out=th[s][:], in_=pg, func=AF.Tanh, scale=0.25),
                  reads=[bpg], writes=[b_th[s]])
            fw.op(V, lambda: nc.vector.scalar_tensor_tensor(out=glo[s][:], in0=th[s][:], scalar=1.0, in1=pv,
                                                            op0=ALU.add, op1=ALU.mult),
                  reads=[b_th[s], bpv], writes=[b_glo[s]])
            fw.dma(fw.sp, glu_d[oc * 128:(oc + 1) * 128, tok0:tok0 + T], glo[s][:], b_glo[s], reads=[b_glo[s]],
                   writes=[c.b_glud[t][oc]])
    fw.barrier()
    P.close()


def a_phase(fw, c, dr, li, x_src, x_dst, glu_d, ntok, seq_len):
    nc = fw.nc
    P = Pool_(nc)
    NQ = 3584
    w = P.t("w_qkvg", [128, KT, NQ], BF16); b_w = Buf("w_qkvg")
    wv_ = dr["w_in"][li].rearrange("(k p) f -> p k f", p=128)
    for k0 in range(0, KT, 2):
        fw.dma(fw.pool, w[:, k0:k0 + 2, :], wv_[:, k0:k0 + 2, 512:4096], b_w, writes=[b_w])
    wos = P.t("w_os", [128, 4, D], BF16); b_wos = Buf("w_os")
    woa = P.t("w_oa", [128, 4, D], BF16); b_woa = Buf("w_oa")
    wo = P.t("w_o", [128, KT, D], BF16); b_wo = Buf("w_o")
    load_weight_bf16(fw, wos, b_wos, dr["w_out_ssm"][li], 4, chunk=4)
    load_weight_bf16(fw, woa, b_woa, dr["w_out_att"][li], 4, chunk=4)
    load_weight_bf16(fw, wo, b_wo, dr["w_o"][li], KT, chunk=4)
    gT = P.t("gT", [128, KT], F32); b_gT = Buf("gT")
    fw.dma(fw.sp, gT[:], dr["gainsT"][li, 2], b_gT, writes=[b_gT])
    gbc = P.t("gbc", [128, D], F32); b_gbc = Buf("gbc")
    fw.dma(fw.sp, gbc[:], dr["gains_bc"][li, 3], b_gbc, writes=[b_gbc])
    R8 = P.t("R8", [128, 8, 641], BF16); b_R8 = Buf("R8")
    Bm = P.t("Bm", [128, 8, 2, 128], BF16); b_Bm = Buf("Bm")
    Q = Pool_(nc)
    rf = Q.t("rf", [128, 8, 641], F32); b_rf = Buf("rf")
    for h0 in range(0, 8, 2):
        fw.dma(fw.sp, rf[:, h0:h0 + 2, :], dr["biasT"][li, h0:h0 + 2].rearrange("h k f -> k h f"), b_rf, writes=[b_rf])
    fw.op(fw.dve, lambda: nc.vector.tensor_scalar(out=R8[:], in0=rf[:], scalar1=8.0, scalar2=None, op0=ALU.mult),
          reads=[b_rf], writes=[b_R8])
    fw.op(fw.dve, lambda: nc.vector.tensor_copy(out=Bm[:, :, 0, :], in_=R8[:, :, 1:129]), reads=[b_R8], writes=[b_Bm])
    fw.op(fw.dve, lambda: nc.vector.tensor_copy(out=Bm[:, :, 1, :], in_=R8[:, :, 513:641]), reads=[b_R8], writes=[b_Bm])
    fw.op(fw.dve, lambda: nc.vector.memset(Bm[64:128, :, 0, 0:64], -240000.0), writes=[b_Bm])
    fw.op(fw.dve, lambda: nc.vector.memset(Bm[0:64, :, 1, 64:128], -240000.0), writes=[b_Bm])
    fw.barrier()
    Q.close()
    ones = P.t("ones", [128, 64], BF16); b_ones = Buf("ones")
    fw.op(fw.dve, lambda: nc.vector.memset(ones[:], 1.0), writes=[b_ones])
    kT = P.t("kTr", [128, 4, 1024], BF16)
    b_kT = [Buf(f"kT{s}") for s in range(2)]
    Vr = P.t("Vr", [128, 8, 512], BF16)
    b_Vr = [Buf(f"Vr{s}") for s in range(8)]
    alloc_norm_bufs(c, P)
    alloc_post_bufs(c, P)
    hT = P.t("hT", [128, KT, T], BF16); b_hT = Buf("hT")
    qT = P.t("qT", [128, 4, T], BF16); b_qT = [Buf(f"qT{k}") for k in range(4)]
    attT = P.t("attT", [128, 4, T], BF16); b_attT = [Buf(f"attT{k}") for k in range(4)]
    gluT = P.t("gluT", [128, 4, T], BF16); b_gluT = Buf("gluT")
    PT = [P.t(f"PT{s}", [128, 640], BF16) for s in range(2)]; b_PT = [Buf(f"PT{s}") for s in range(2)]
    rden = [P.t(f"rden{s}", [128, 128], F32) for s in range(2)]; b_rden = [Buf(f"rden{s}") for s in range(2)]
    ta = P.t("ta", [128, T], F32); b_ta = Buf("ta")
    tb = P.t("tb", [128, T], F32); b_tb = Buf("tb")
    m1 = P.t("m1", [128, T], F32); b_m1 = Buf("m1")
    m2 = P.t("m2", [128, T], F32); b_m2 = Buf("m2")
    mrg = P.t("mrg", [128, KT, T], BF16); b_mrg = [Buf(f"mrg{k}") for k in range(KT)]
    ps, b_ps = c.ps, c.b_ps
    ntile = ntok // T
    tps = seq_len // T
    V = fw.dve
    si = 0
    ndi = 0
    for t in range(ntile):
        tok0 = t * T
        tl = t % tps
        half = tl % 2
        for blk in range(NB):
            norm_block(fw, c, x_src, tok0 + blk * 128, blk)
            transpose_block(fw, c, blk, blk, hT, b_hT, gT, b_gT)
        fw.dma(fw.sp, gluT[:], glu_d[:, tok0:tok0 + T].rearrange("(k p) t -> p k t", p=128), b_gluT,
               reads=c.b_glud[t], writes=[b_gluT])
        for hp in range(4):
            for which in range(2):
                pq, bpq = ps[0][:, which * 512:(which + 1) * 512], b_ps[which]
                c0 = which * 512 + hp * 128
                for k in range(KT):
                    fw.op(fw.pe, lambda: nc.tensor.matmul(pq, lhsT=w[:, k, c0:c0 + 128], rhs=hT[:, k, :],
                                                          start=(k == 0), stop=(k == KT - 1)),
                          reads=[b_w, b_hT], writes=[bpq], flag=(k == KT - 1))
                if which == 0:
                    fw.op(fw.act, lambda: nc.scalar.activation(out=qT[:, hp, :], in_=pq, func=AF.Copy),
                          reads=[bpq], writes=[b_qT[hp]])
                else:
                    fw.op(fw.act, lambda: nc.scalar.activation(out=kT[:, hp, half * 512:(half + 1) * 512], in_=pq,
                                                               func=AF.Copy),
                          reads=[bpq], writes=[b_kT[half]])
        for blk in range(NB):
            pv, bpv = ps[3][:, (blk % 2) * 512:(blk % 2 + 1) * 512], b_ps[6 + blk % 2]
            for k in range(KT):
                fw.op(fw.pe, lambda: nc.tensor.matmul(pv, lhsT=hT[:, k, blk * 128:(blk + 1) * 128], rhs=w[:, k, 1024:1536],
                                                      start=(k == 0), stop=(k == KT - 1)),
                      reads=[b_w, b_hT], writes=[bpv], flag=(k == KT - 1))
            slot = half * 4 + blk
            fw.op(fw.act, lambda: nc.scalar.activation(out=Vr[:, slot, :], in_=pv, func=AF.Copy),
                  reads=[bpv], writes=[b_Vr[slot]])
        for blk in range(NB):
            jb = tl * 4 + blk
            deltas = [d_ for d_ in range(5) if jb - d_ >= 0]
            nd = len(deltas)
            for hp in range(4):
                pnd, bpnd = ps[3][:, (ndi % 2) * 512:(ndi % 2) * 512 + 256], b_ps[6 + ndi % 2]
                ndi += 1
                for e in range(2):
                    hd = 2 * hp + e
                    r0 = e * 64
                    s = si % 2
                    si += 1
                    pS = ps[1 + s]
                    bS = [b_ps[2 + 2 * s], b_ps[3 + 2 * s]]
                    for di, d_ in enumerate(deltas):
                        kb = jb - d_
                        kc = (kb % 8) * 128
                        o = pS[:, di * 128:(di + 1) * 128]
                        bo = bS[0] if di < 4 else bS[1]
                        fw.op(fw.pe, lambda: nc.tensor.matmul(o, lhsT=kT[r0:r0 + 64, hp, kc:kc + 128],
                                                              rhs=qT[r0:r0 + 64, hp, blk * 128:(blk + 1) * 128],
                                                              start=True, stop=False),
                              reads=[b_kT[(kb % 8) // 4], b_qT[hp]], writes=[bo], flag=False)
                        if d_ == 0:
                            bt = Bm[:, hd, 0, :]
                        elif d_ == 4:
                            bt = Bm[:, hd, 1, :]
                        else:
                            bt = R8[:, hd, 1 + 128 * d_:129 + 128 * d_]
                        fw.op(fw.pe, lambda: nc.tensor.matmul(o, lhsT=c.ident[:], rhs=bt, start=False, stop=True),
                              reads=[b_R8, b_Bm, c.b_ident], writes=[bo], flag=(di == nd - 1 or di == 3))
                    fw.op(fw.act, lambda: nc.scalar.activation(out=PT[s][:, 0:nd * 128], in_=pS[:, 0:nd * 128],
                                                               func=AF.Exp, scale=0.125),
                          reads=bS, writes=[b_PT[s]])
                    for di, d_ in enumerate(deltas):
                        kb = jb - d_
                        fw.op(fw.pe, lambda: nc.tensor.matmul(pnd[r0:r0 + 64, 0:128],
                                                              lhsT=Vr[:, kb % 8, hd * 64:(hd + 1) * 64],
                                                              rhs=PT[s][:, di * 128:(di + 1) * 128],
                                                              start=(di == 0), stop=(di == nd - 1)),
                              reads=[b_Vr[kb % 8], b_PT[s]], writes=[bpnd], flag=False)
                    for di, d_ in enumerate(deltas):
                        fw.op(fw.pe, lambda: nc.tensor.matmul(pnd[r0:r0 + 64, 128:256], lhsT=ones[:],
                                                              rhs=PT[s][:, di * 128:(di + 1) * 128],
                                                              start=(di == 0), stop=(di == nd - 1)),
                              reads=[b_ones, b_PT[s]], writes=[bpnd], flag=(di == nd - 1))
                rs_ = hp % 2
                fw.op(V, lambda: nc.vector.reciprocal(out=rden[rs_][:], in_=pnd[:, 128:256]),
                      reads=[bpnd], writes=[b_rden[rs_]])
                fw.op(V, lambda: nc.vector.tensor_tensor(out=attT[:, hp, blk * 128:(blk + 1) * 128], in0=pnd[:, 0:128],
                                                         in1=rden[rs_][:], op=ALU.mult),
                      reads=[bpnd, b_rden[rs_]], writes=[b_attT[hp]])
        for cc in range(KT):
            pga, bpga = ps[1][:, 0:512], b_ps[2]
            pgb, bpgb = ps[1][:, 512:1024], b_ps[3]
            pya, bpya = ps[2][:, 0:512], b_ps[4]
            pyb, bpyb = ps[2][:, 512:1024], b_ps[5]
            for k in range(KT):
                fw.op(fw.pe, lambda: nc.tensor.matmul(pga, lhsT=w[:, k, 1536 + cc * 128:1536 + (cc + 1) * 128],
                                                      rhs=hT[:, k, :], start=(k == 0), stop=(k == KT - 1)),
                      reads=[b_w, b_hT], writes=[bpga], flag=(k == KT - 1))
            for k in range(KT):
                fw.op(fw.pe, lambda: nc.tensor.matmul(pgb, lhsT=w[:, k, 2560 + cc * 128:2560 + (cc + 1) * 128],
                                                      rhs=hT[:, k, :], start=(k == 0), stop=(k == KT - 1)),
                      reads=[b_w, b_hT], writes=[bpgb], flag=(k == KT - 1))
            for k in range(4):
                fw.op(fw.pe, lambda: nc.tensor.matmul(pya, lhsT=wos[:, k, cc * 128:(cc + 1) * 128], rhs=gluT[:, k, :],
                                                      start=(k == 0), stop=(k == 3)),
                      reads=[b_wos, b_gluT], writes=[bpya], flag=(k == 3))
            for k in range(4):
                fw.op(fw.pe, lambda: nc.tensor.matmul(pyb, lhsT=woa[:, k, cc * 128:(cc + 1) * 128], rhs=attT[:, k, :],
                                                      start=(k == 0), stop=(k == 3)),
                      reads=[b_woa, b_attT[k]], writes=[bpyb], flag=(k == 3))
            fw.op(fw.act, lambda: nc.scalar.activation(out=ta[:], in_=pga, func=AF.Tanh, scale=0.5),
                  reads=[bpga], writes=[b_ta])
            fw.op(fw.act, lambda: nc.scalar.activation(out=tb[:], in_=pgb, func=AF.Tanh, scale=0.5),
                  reads=[bpgb], writes=[b_tb])
            fw.op(V, lambda: nc.vector.scalar_tensor_tensor(out=m1[:], in0=ta[:], scalar=1.0, in1=pya,
                                                            op0=ALU.add, op1=ALU.mult),
                  reads=[b_ta, bpya], writes=[b_m1])
            fw.op(V, lambda: nc.vector.scalar_tensor_tensor(out=m2[:], in0=tb[:], scalar=1.0, in1=pyb,
                                                            op0=ALU.add, op1=ALU.mult),
                  reads=[b_tb, bpyb], writes=[b_m2])
            fw.op(V, lambda: nc.vector.scalar_tensor_tensor(out=mrg[:, cc, :], in0=m1[:], scalar=0.25, in1=m2[:],
                                                            op0=ALU.mult, op1=ALU.add),
                  reads=[b_m1, b_m2], writes=[b_mrg[cc]])
        for blk in range(NB):
            pi_ = 3 if blk % 2 == 0 else 0
            po = ps[pi_]
            bpo = [b_ps[2 * pi_], b_ps[2 * pi_ + 1]]
            for hf in range(2):
                for cc in range(KT):
                    fw.op(fw.pe, lambda: nc.tensor.matmul(po[:, hf * 512:(hf + 1) * 512],
                                                          lhsT=mrg[:, cc, blk * 128:(blk + 1) * 128],
                                                          rhs=wo[:, cc, hf * 512:(hf + 1) * 512],
                                                          start=(cc == 0), stop=(cc == KT - 1)),
                          reads=[b_wo, b_mrg[cc]], writes=[bpo[hf]], flag=(cc == KT - 1))
            postnorm_residual(fw, c, po[:], bpo, x_src, x_dst, tok0 + blk * 128, gbc, b_gbc, 1.0, 4.0 * EPS)
    fw.barrier()
    P.close()


def build_program(ntok, layers, phases=("F1", "S", "A", "F2"), seq_len=4096):
    nc = bass.Bass("TRN2", target_bir_lowering=False)
    L = len(layers)
    shapes = {
        "x": ([ntok, D], F32), "ident": ([128, 128], BF16),
        "ffn_w_gate": ([L, 2, D, DFF], F32), "ffn_w_up": ([L, 2, D, DFF], F32), "ffn_w_down": ([L, 2, DFF, D], F32),
        "gainsT": ([L, 6, 128, KT], F32), "gains_bc": ([L, 6, 128, D], F32),
        "w_in": ([L, D, 4096], F32), "w_glu_val": ([L, 512, 512], F32), "w_glu_gate": ([L, 512, 512], F32),
        "lamre_s": ([L, 128, 16], F32), "lamim_s": ([L, 128, 16], F32), "logdt_s": ([L, 128, 16], F32),
        "bre_s": ([L, 128, 16, 16], F32), "bim_s": ([L, 128, 16, 16], F32),
        "cre_s": ([L, 128, 16, 16], F32), "cim_s": ([L, 128, 16, 16], F32),
        "d_s": ([L, 128, 4], F32), "w_out_ssm": ([L, 512, D], F32), "w_out_att": ([L, 512, D], F32),
        "w_o": ([L, D, D], F32), "biasT": ([L, 8, 128, 641], F32), "iota": ([128, 129], F32),
        "ident_f": ([128, 128], F32),
    }

    class LazyIn(dict):
        def __missing__(self, name):
            shp, dt = shapes[name]
            self[name] = nc.dram_tensor(name, list(shp), dt, kind="ExternalInput").ap()
            return self[name]
    dr = LazyIn()
    x_in = dr["x"]
    dbg = os.environ.get("K_DEBUG_GLU")
    glu_d = nc.dram_tensor("glu_d", [512, ntok], BF16, kind=("ExternalOutput" if dbg else "Internal")).ap()
    y = nc.dram_tensor("y", [ntok, D], F32, kind="ExternalOutput").ap()

    with ExitStack() as es:
        fw = FW(nc, es)
        c = Ctx()
        c.b_xdram = {}
        c.ps = [es.enter_context(nc.psum_tensor(f"ps{i}", [128, 1024], F32)) for i in range(4)]
        c.b_ps = [Buf(f"psb{i}") for i in range(8)]
        c.ident = es.enter_context(nc.sbuf_tensor("ident_sb", [128, 128], BF16))
        c.b_ident = Buf("ident")
        fw.dma(fw.sp, c.ident[:], dr["ident"], c.b_ident, writes=[c.b_ident])
        cur = x_in
        for li in range(L):
            for ph in phases:
                if ph in ("F1", "F2"):
                    i = 0 if ph == "F1" else 1
                    ffn_phase(fw, c, cur, y, dr["ffn_w_gate"][li, i], dr["ffn_w_up"][li, i],
                              dr["ffn_w_down"][li, i], dr["gainsT"][li, 4 * i], dr["gains_bc"][li, 4 * i + 1], ntok)
                    cur = y
                elif ph == "S":
                    c.b_glud = [[Buf(f"glud{t}_{o}") for o in range(4)] for t in range(ntok // T)]
                    s_phase(fw, c, dr, li, cur, glu_d, ntok, seq_len)
                elif ph == "A":
                    a_phase(fw, c, dr, li, cur, y, glu_d, ntok, seq_len)
                    cur = y
        fw.finish(fw.sp)
        print(f"[build] instructions={fw.ninst} waits={fw.nwaits} sems={fw.nsem}")
    nc.declared_inputs = sorted(dr.keys())
    return nc


def host_inputs(inputs, layers):
    import ml_dtypes
    g = np.asarray(inputs["norm_gains"], np.float32)[layers]
    L = len(layers)
    out = {
        "ident": np.eye(128, dtype=np.float32).astype(ml_dtypes.bfloat16),
        "ffn_w_gate": np.ascontiguousarray(inputs["ffn_w_gate"][layers]),
        "ffn_w_up": np.ascontiguousarray(inputs["ffn_w_up"][layers]),
        "ffn_w_down": np.ascontiguousarray(inputs["ffn_w_down"][layers]),
        "gainsT": np.ascontiguousarray(g.reshape(L, 6, KT, 128).transpose(0, 1, 3, 2)),
        "gains_bc": np.ascontiguousarray(np.broadcast_to(g[:, :, None, :], (L, 6, 128, D))),
    }
    G_, P_, H_ = 32, 64, 16

    def st(a):
        a = np.asarray(a, np.float32)[layers]
        return np.ascontiguousarray(a.reshape(L, 16, 2, P_).transpose(0, 2, 3, 1).reshape(L, 128, 16))
    out["lamre_s"] = st(inputs["lam_re"])
    out["lamim_s"] = st(inputs["lam_im"])
    out["logdt_s"] = st(np.broadcast_to(np.asarray(inputs["log_dt"], np.float32)[:, :, None], (len(inputs["log_dt"]), G_, P_)))
    for nm, key in (("bre_s", "b_re"), ("bim_s", "b_im")):
        a = np.asarray(inputs[key], np.float32)[layers]
        out[nm] = np.ascontiguousarray(a.reshape(L, 16, 2, P_, H_).transpose(0, 2, 3, 1, 4).reshape(L, 128, 16, H_))
    for nm, key in (("cre_s", "c_re"), ("cim_s", "c_im")):
        a = np.asarray(inputs[key], np.float32)[layers]
        out[nm] = np.ascontiguousarray(a.reshape(L, 16, 2, H_, P_).transpose(0, 2, 4, 1, 3).reshape(L, 128, 16, H_))
    d = np.asarray(inputs["d_skip"], np.float32)[layers]
    out["d_s"] = np.ascontiguousarray(d.reshape(L, 4, 128).transpose(0, 2, 1))
    out["iota"] = np.ascontiguousarray(np.broadcast_to(np.arange(129, dtype=np.float32)[None, :], (128, 129)))
    out["ident_f"] = np.eye(128, dtype=np.float32)
    rb = np.asarray(inputs["rel_bias"], np.float32)[layers]
    rb_ext = np.concatenate([rb, np.repeat(rb[:, :, -1:], 768 - 257, axis=2)], axis=2)
    idx = (127 - np.arange(128))[:, None] + np.arange(641)[None, :]
    out["biasT"] = np.ascontiguousarray(rb_ext[:, :, idx])
    for k in ("w_in", "w_glu_val", "w_glu_gate", "w_out_ssm", "w_out_att", "w_o"):
        out[k] = np.ascontiguousarray(np.asarray(inputs[k], np.float32)[layers])
    return out


_PROG_CACHE = {}
LAYERS_PER_LAUNCH = 1


def _get_program(ntok, nlayers, seq_len):
    key = (ntok, nlayers, seq_len)
    if key not in _PROG_CACHE:
        _PROG_CACHE[key] = build_program(ntok, list(range(nlayers)), seq_len=seq_len)
    return _PROG_CACHE[key]


def kernel(**inputs):
    x = np.ascontiguousarray(np.asarray(inputs["x"], np.float32))
    B, S, _ = x.shape
    per = B // N_CORES
    ntok = per * S
    depth = int(np.asarray(inputs["norm_gains"]).shape[0])
    lpl = LAYERS_PER_LAUNCH
    shards = [x[i * per:(i + 1) * per].reshape(ntok, D) for i in range(N_CORES)]
    for l0 in range(0, depth, lpl):
        layers = list(range(l0, l0 + lpl))
        nc = _get_program(ntok, lpl, S)
        shared = host_inputs(inputs, layers)
        in_maps = []
        for i in range(N_CORES):
            m = {k: v for k, v in shared.items() if k in nc.declared_inputs}
            m["x"] = shards[i]
            in_maps.append(m)
        res = run_bass_kernel_spmd(nc, in_maps, core_ids=list(range(N_CORES)))
        shards = [np.asarray(res.results[i]["y"], np.float32) for i in range(N_CORES)]
    out = np.stack([s.reshape(per, S, D) for s in shards], axis=0).reshape(B, S, D)
    return out.astype(np.float32)
```
